# Optimizing a Trainium2 kernel written in Bass

```python
import math
import jax, jax.numpy as jnp
from jax import lax
import numpy as np

D_MODEL = 4096
BATCH = 2
SEQ = 8192
DEPTH = 1

D_MIX = D_MODEL
D_ATTN = D_MIX // 2
HEAD_DIM = 128
N_HEADS = D_ATTN // HEAD_DIM
D_POOL = D_MIX - D_ATTN
POOL_SIZES = (2, 4, 8, 16)
N_POOL_GROUPS = len(POOL_SIZES)
POOL_GROUP_DIM = D_POOL // N_POOL_GROUPS
DILATED_PATTERNS = ((128, 1), (512, 4), (2048, 16))
Q_BLOCK = 128
EPS = 1e-6

kernel_name = "hybrid_dilated_attn_pool_adaln_block"


def rmsnorm(x, gain):
    xf = x.astype(jnp.float32)
    r = lax.rsqrt(jnp.mean(xf * xf, axis=-1, keepdims=True) + EPS)
    return (xf * r * gain.astype(jnp.float32)).astype(x.dtype)


def alibi_slopes(n_heads):
    return 2.0 ** (-8.0 * jnp.arange(1, n_heads + 1, dtype=jnp.float32) / n_heads)


def dilated_window_attention(q, k, v, window, dilation, slopes):
    B, S, H, E = q.shape
    span = window // dilation
    assert span <= Q_BLOCK
    L = S // dilation
    nb = -(-L // Q_BLOCK)
    Lp = nb * Q_BLOCK

    def to_residue(t):
        return t.reshape(B, L, dilation, H, E).transpose(0, 2, 3, 1, 4)

    z3 = ((0, 0), (0, 0), (0, 0))
    qr = jnp.pad(to_residue(q), z3 + ((0, Lp - L), (0, 0)))
    kr = jnp.pad(to_residue(k), z3 + ((Q_BLOCK, Lp - L), (0, 0)))
    vr = jnp.pad(to_residue(v), z3 + ((Q_BLOCK, Lp - L), (0, 0)))
    qb = qr.reshape(B, dilation, H, nb, Q_BLOCK, E)
    kb = kr.reshape(B, dilation, H, nb + 1, Q_BLOCK, E)
    vb = vr.reshape(B, dilation, H, nb + 1, Q_BLOCK, E)
    kb = jnp.concatenate([kb[:, :, :, :-1], kb[:, :, :, 1:]], axis=-2)
    vb = jnp.concatenate([vb[:, :, :, :-1], vb[:, :, :, 1:]], axis=-2)

    qi = jnp.arange(Q_BLOCK)[:, None]
    ki = jnp.arange(2 * Q_BLOCK)[None, :]
    rel = Q_BLOCK + qi - ki
    key_idx = jnp.arange(nb)[:, None, None] * Q_BLOCK + ki[None] - Q_BLOCK
    valid = (rel >= 0)[None] & (rel <= span)[None] & (key_idx >= 0)
    bias = -slopes[:, None, None] * (rel * dilation).astype(jnp.float32)[None]

    scale = 1.0 / math.sqrt(E)
    s = jnp.einsum('brhnqe,brhnke->brhnqk', qb, kb).astype(jnp.float32) * scale
    s = s + bias[:, None]
    s = jnp.where(valid, s, -jnp.inf)
    lse = jax.nn.logsumexp(s, axis=-1)
    p = jnp.exp(s - lse[..., None])
    o = jnp.einsum('brhnqk,brhnke->brhnqe', p.astype(v.dtype), vb)
    o = o.reshape(B, dilation, H, Lp, E)[:, :, :, :L]
    lse = lse.reshape(B, dilation, H, Lp)[:, :, :, :L]
    o = o.transpose(0, 3, 1, 2, 4).reshape(B, S, H, E)
    lse = lse.transpose(0, 3, 1, 2).reshape(B, S, H)
    return o, lse


def longnet_mixture(q, k, v):
    slopes = alibi_slopes(q.shape[2])
    outs, lses = [], []
    for window, dilation in DILATED_PATTERNS:
        o, lse = dilated_window_attention(q, k, v, window, dilation, slopes)
        outs.append(o)
        lses.append(lse)
    w = jax.nn.softmax(jnp.stack(lses, axis=0), axis=0)
    o = jnp.stack(outs, axis=0).astype(jnp.float32)
    return jnp.sum(w[..., None] * o, axis=0).astype(q.dtype)


def multiscale_pool(u, w_pool, pool_scale):
    B, S, _ = u.shape
    ug = u.reshape(B, S, N_POOL_GROUPS, POOL_GROUP_DIM)
    outs = []
    for g, w in enumerate(POOL_SIZES):
        xg = ug[:, :, g].astype(jnp.float32)
        cs = jnp.pad(jnp.cumsum(xg, axis=1), ((0, 0), (1, 0), (0, 0)))
        lagged = jnp.pad(cs, ((0, 0), (w - 1, 0), (0, 0)))[:, :S]
        count = jnp.minimum(jnp.arange(1, S + 1), w).astype(jnp.float32)[None, :, None]
        pooled = (cs[:, 1:] - lagged) / count - xg
        outs.append(jnp.einsum('bsc,cd->bsd', pooled.astype(u.dtype), w_pool[g]))
    return jnp.concatenate(outs, axis=-1) * pool_scale


def setup_inputs(seed: int = 0) -> dict:
    key = jax.random.key(seed)
    ks = jax.random.split(key, 12)
    f32 = jnp.float32
    x = jax.random.normal(ks[0], (BATCH, SEQ, D_MODEL), f32)
    c = jax.random.normal(ks[1], (BATCH, D_MODEL), f32)
    norm_gain = 1.0 + 0.05 * jax.random.normal(ks[2], (DEPTH, D_MODEL), f32)
    w_ada = jax.random.normal(ks[3], (DEPTH, D_MODEL, 3 * D_MODEL), f32) * D_MODEL ** -0.5
    b_ada = 0.02 * jax.random.normal(ks[4], (DEPTH, 3 * D_MODEL), f32)
    n_in = 4 * D_ATTN + 2 * D_POOL
    w_in = jax.random.normal(ks[5], (DEPTH, D_MODEL, n_in), f32) * D_MODEL ** -0.5
    w_pool = jax.random.normal(ks[6], (DEPTH, N_POOL_GROUPS, POOL_GROUP_DIM, POOL_GROUP_DIM), f32) * POOL_GROUP_DIM ** -0.5
    pool_scale = 0.5 + 0.1 * jax.random.normal(ks[7], (DEPTH, D_POOL), f32)
    w_out = jax.random.normal(ks[8], (DEPTH, D_MIX, D_MODEL), f32) * D_MIX ** -0.5
    final_gain = 1.0 + 0.05 * jax.random.normal(ks[9], (D_MODEL,), f32)
    return {"x": x, "c": c, "norm_gain": norm_gain, "w_ada": w_ada, "b_ada": b_ada,
            "w_in": w_in, "w_pool": w_pool, "pool_scale": pool_scale,
            "w_out": w_out, "final_gain": final_gain}


def reference(x, c, norm_gain, w_ada, b_ada, w_in, w_pool, pool_scale, w_out, final_gain):
    B, S, _ = x.shape
    c_act = jax.nn.silu(c)
    for l in range(DEPTH):
        mod = c_act @ w_ada[l] + b_ada[l]
        shift, scale, gate = jnp.split(mod, 3, axis=-1)
        h = rmsnorm(x, norm_gain[l])
        h = h * (1.0 + scale[:, None, :]) + shift[:, None, :]
        proj = jnp.einsum('bsd,dn->bsn', h, w_in[l])
        q, k, v, g_attn, u_pool, g_pool = jnp.split(
            proj, np.cumsum([D_ATTN, D_ATTN, D_ATTN, D_ATTN, D_POOL]).tolist(), axis=-1)
        q = q.reshape(B, S, N_HEADS, HEAD_DIM)
        k = k.reshape(B, S, N_HEADS, HEAD_DIM)
        v = v.reshape(B, S, N_HEADS, HEAD_DIM)
        y_attn = longnet_mixture(q, k, v).reshape(B, S, D_ATTN) * jax.nn.silu(g_attn)
        y_pool = multiscale_pool(u_pool, w_pool[l], pool_scale[l]) * jax.nn.silu(g_pool)
        y = jnp.concatenate([y_attn, y_pool], axis=-1) @ w_out[l]
        x = x + gate[:, None, :] * y
    return rmsnorm(x, final_gain)
```

```python
import math
import numpy as np
import concourse.bass as bass
import concourse.mybir as mybir
from concourse.bass_utils import run_bass_kernel_spmd

F32 = mybir.dt.float32
BF16 = mybir.dt.bfloat16
ALU = mybir.AluOpType
AF = mybir.ActivationFunctionType
AX = mybir.AxisListType

ENGS = ("tensor", "vector", "scalar", "gpsimd", "sync")
EPS = 1e-6
NEG = -30000.0
PATTERNS = (1, 4, 16)


class Res:
    __slots__ = ("name", "w", "r")

    def __init__(self, name):
        self.name = name
        self.w = None
        self.r = {}


class Sched:
    def __init__(self, eng_sems, dma_sems, gdma_sems):
        self.q = {e: [] for e in ENGS}
        self.sem = dict(eng_sems)
        for i, s in enumerate(dma_sems):
            self.sem[("dma", i)] = s
        for i, s in enumerate(gdma_sems):
            self.sem[("gdma", i)] = s
        self.cnt = {k: 0 for k in self.sem}
        self.seen = {e: {} for e in ENGS}
        self.nwait = 0

    def _wait(self, eng, k, v):
        seen = self.seen[eng]
        if seen.get(k, 0) < v:
            seen[k] = v
            h = self.sem[k]
            self.q[eng].append(lambda e, h=h, v=v: e.wait_ge(h, v))
            self.nwait += 1

    def _deps(self, eng, reads, writes):
        need = {}
        for r in reads:
            if r.w is not None:
                k, v = r.w
                if need.get(k, 0) < v:
                    need[k] = v
        for w in writes:
            if w.w is not None:
                k, v = w.w
                if need.get(k, 0) < v:
                    need[k] = v
            for k, v in w.r.items():
                if need.get(k, 0) < v:
                    need[k] = v
        for k, v in need.items():
            self._wait(eng, k, v)

    def _mark(self, tick, reads, writes):
        k, v = tick
        for r in reads:
            if r.r.get(k, 0) < v:
                r.r[k] = v
        for w in writes:
            w.w = tick
            w.r = {}

    def op(self, eng, fn, reads=(), writes=()):
        self._deps(eng, reads, writes)
        self.cnt[eng] += 1
        v = self.cnt[eng]
        h = self.sem[eng]
        self.q[eng].append(lambda e, fn=fn, h=h: fn(e).then_inc(h, 1))
        self._mark((eng, v), reads, writes)

    def dma(self, eng, out, in_, reads=(), writes=(), ch=0):
        self._deps(eng, reads, writes)
        k = ("gdma" if eng == "gpsimd" else "dma", ch)
        self.cnt[k] += 16
        v = self.cnt[k]
        h = self.sem[k]
        self.q[eng].append(lambda e, out=out, in_=in_, h=h: e.dma_start(out=out, in_=in_).then_inc(h, 16))
        self._mark((k, v), reads, writes)

    def barrier(self):
        for eng in ENGS:
            for k, v in self.cnt.items():
                if v > 0:
                    self._wait(eng, k, v)

    def final_wait(self, eng, ress):
        self._deps(eng, ress, ())

    def emit(self, block):
        for name in ENGS:
            lst = self.q[name]

            def body(e, lst=lst):
                for th in lst:
                    th(e)
            getattr(block, name)(body)


class Region:
    def __init__(self, nc, start, end):
        self.nc, self.off, self.end = nc, start, end
        self.n = 0

    def alloc(self, name, shape, dt, at=None):
        esz = 4 if dt == F32 else 2
        nbytes = int(np.prod(shape[1:])) * esz
        if at is not None:
            Region.uid = getattr(Region, "uid", 0) + 1
            return self.nc.alloc_sbuf_tensor_at("%s_%d" % (name, Region.uid), list(shape), dt, offset=at)
        off = (self.off + 31) // 32 * 32
        self.last = off
        assert off + nbytes <= self.end, (name, off, nbytes, self.end)
        Region.uid = getattr(Region, "uid", 0) + 1
        t = self.nc.alloc_sbuf_tensor_at("%s_%d" % (name, Region.uid), list(shape), dt, offset=off)
        self.off = off + nbytes
        return t


class Cfg:
    def __init__(self, D=4096, NH=16, GP=512, T=2048, HALO=2048, NR=4, NG=2):
        self.D, self.NH, self.GP, self.T, self.HALO = D, NH, GP, T, HALO
        self.NR, self.NG = NR, NG
        self.CPR = 3 * (D // 128) // NR
        self.KC = D // 128
        self.DA = NH * 128
        self.DP = 4 * GP
        self.CPG = GP // 128
        self.NIN = 4 * self.DA + 2 * self.DP
        self.NCB = self.NIN // 128
        self.DMIX = self.DA + self.DP
        self.MC = self.DMIX // 128
        self.TGS = 512
        self.NTGO = T // self.TGS
        self.NTGH = HALO // self.TGS
        self.NTG = self.NTGO + self.NTGH
        self.TT = HALO + T
        self.D3 = 3 * D
        self.CW = 2048 if self.D3 % 2048 == 0 else 1024
        self.NPIECE = self.D3 // self.CW
        self.CGW = min(512, D)
        self.NCG = D // self.CGW
        self.NT = T // 128


def build_nc(cfg):
    c = cfg
    D, NH, KC, T, HALO, TT, TGS, MC, CPG = c.D, c.NH, c.KC, c.T, c.HALO, c.TT, c.TGS, c.MC, c.CPG
    nc = bass.Bass("TRN2", target_bir_lowering=False)
    dr = lambda name, shape, dt=F32, kind="ExternalInput": nc.dram_tensor(name, list(shape), dt, kind=kind).ap()
    x_d = dr("x", [TT, D])
    cT_d = dr("cT", [128, KC])
    CPR, NR = c.CPR, c.NR
    wada_d = dr("w_ada_q", [D, CPR * 128])
    bcol_d = dr("b_ada_col", [128, CPR])
    ccin_d = dr("cc_in", [128, CPR], F32, "Internal")
    ccout_d = dr("cc_out", [NR * 128, CPR], F32, "Internal")
    gaincol_d = dr("gain_col", [128, KC])
    win_d = dr("w_in_r", [c.NCB, 128, KC * 128])
    wpool_d = dr("w_pool_r", [4, 128, CPG * CPG * 128])
    pscale_d = dr("pscale_col", [128, c.DP // 128])
    wout_d = dr("w_out_r", [c.NCG, 128, MC * c.CGW])
    fgain_d = dr("fgain_bc", [128, D])
    hflag_d = dr("hflag", [128, 2])
    invcnt_d = dr("invcnt", [128, 64])
    relT_d = dr("relT", [128, 256])
    maskT_d = dr("maskT", [128, 256])
    ident_d = dr("ident", [128, 128])
    tabs_d = dr("tabs", [NH, 128, 3 * 768], BF16)
    out_d = dr("out", [T, D], F32, "ExternalOutput")
    hscr_d = dr("hT_scr", [c.NTG, 128, KC * TGS], BF16, "Internal")
    yscr_d = dr("yT_scr", [c.NTGO, 128, MC * TGS], BF16, "Internal")
    gscr_d = dr("gate_scr", [128, D], F32, "Internal")

    SB_END = 224 * 1024
    banks = [nc.alloc_psum_tensor("bank%d" % i, [128, 512], F32) for i in range(8)]
    RB = [Res("bank%d" % i) for i in range(8)]

    def bank_bf(i):
        return banks[i][:].bitcast(BF16)

    from contextlib import ExitStack
    with ExitStack() as st:
        eng_sems = {e: st.enter_context(nc.semaphore("s_" + e)) for e in ENGS}
        dma_sems = [st.enter_context(nc.semaphore("d%d" % i)) for i in range(12)]
        gdma_sems = [st.enter_context(nc.semaphore("g%d" % i)) for i in range(20)]
        cc_sem = st.enter_context(nc.semaphore("ccsem"))
        block = st.enter_context(nc.Block())
        S = Sched(eng_sems, dma_sems, gdma_sems)
        S.sem["cc"] = cc_sem
        S.cnt["cc"] = 0

        SB_BASE = 17408
        P = Region(nc, SB_BASE, SB_BASE + 8 * 1024)
        ident_bf = P.alloc("ident", [128, 128], BF16)
        ones_f = P.alloc("ones_f", [128, 128], F32)
        ident_f = P.alloc("ident_f", [128, 128], F32)
        ones_bf = P.alloc("ones_bf", [128, 128], BF16)
        modcol = P.alloc("modcol", [128, 2 * KC], F32)
        Gcol = P.alloc("Gcol", [128, KC], F32)
        hflag = P.alloc("hflag", [128, 2], F32)
        invcnt = P.alloc("invcnt", [128, 64], F32)
        relT = P.alloc("relT", [128, 256], F32)
        maskT = P.alloc("maskT", [128, 256], F32)
        hmfull = P.alloc("hmfull", [128, 256], F32)
        pscale = P.alloc("pscale", [128, c.DP // 128], F32)
        PH0 = P.off + 32
        R_const = Res("const")
        R_ident, R_ones, R_modcol, R_Gcol, R_hm = Res("ident"), Res("ones"), Res("modcol"), Res("Gcol"), Res("hm")

        CH_CONST = 0
        for dst, src in ((hflag, hflag_d), (invcnt, invcnt_d), (relT, relT_d), (maskT, maskT_d), (pscale, pscale_d)):
            S.dma("sync", dst[:], src, writes=[R_const], ch=CH_CONST)
        S.dma("gpsimd", ident_bf[:], ident_d, writes=[R_ident], ch=0)
        R_identf = Res("identf")
        S.dma("sync", ident_f[:], ident_d, writes=[R_identf], ch=6)
        S.op("vector", lambda e: e.memset(ones_f[:], 1.0), writes=[R_ones])
        S.op("vector", lambda e: e.memset(ones_bf[:], 1.0), writes=[R_ones])
        S.op("vector", lambda e: e.memset(hmfull[:], 0.0), writes=[R_hm])
        S.op("vector", lambda e: e.tensor_scalar(out=hmfull[:, 0:128], in0=hmfull[:, 0:128], scalar1=hflag[:, 1:2],
                                                 scalar2=None, op0=ALU.add), reads=[R_const], writes=[R_hm])

        A = Region(nc, PH0, SB_END)
        CWQ = CPR * 128
        acc = A.alloc("acc", [128, CWQ], F32)
        NWP = 4
        wp = [A.alloc("wp%d" % i, [128, CWQ], F32) for i in range(NWP)]
        cT = A.alloc("cT", [128, KC], F32)
        cact = A.alloc("cact", [128, KC], F32)
        bcol = A.alloc("bcol", [128, CPR], F32)
        modq = A.alloc("modq", [128, CPR], F32)
        allcol = A.alloc("allcol", [128, NR * CPR], F32)
        gaincol = A.alloc("gaincol", [128, KC], F32)
        gate_sb = A.alloc("gate_sb", [128, D], F32)
        dg = [A.alloc("dg%d" % i, [128, 128], F32) for i in range(2)]
        R_wp = [Res("wp%d" % i) for i in range(NWP)]
        R_acc = [Res("acc")]
        R_c0, R_cact, R_gate = Res("c0"), Res("cact"), Res("gate_sb")
        R_gscr, R_modq, R_ccin, R_ccout, R_allcol = Res("gscr"), Res("modq"), Res("ccin"), Res("ccout"), Res("allcol")
        R_dg = [Res("dg0"), Res("dg1")]
        CH_WP = 1
        for dst, src in ((cT, cT_d), (bcol, bcol_d), (gaincol, gaincol_d)):
            S.dma("sync", dst[:], src, writes=[R_c0], ch=5)
        S.op("scalar", lambda e: e.activation(out=cact[:], in_=cT[:], func=AF.Silu), reads=[R_c0], writes=[R_cact])
        for kt in range(KC):
            s = kt % NWP
            S.dma("sync", wp[s][:], wada_d[kt * 128:(kt + 1) * 128, :], writes=[R_wp[s]], ch=CH_WP + s)
            if kt == 0:
                S.op("vector", lambda e, s=s: e.tensor_scalar(
                    out=acc[:], in0=wp[s][:], scalar1=cact[:, 0:1], scalar2=None, op0=ALU.mult),
                    reads=[R_wp[s], R_cact], writes=R_acc)
            else:
                S.op("vector", lambda e, s=s, kt=kt: e.scalar_tensor_tensor(
                    out=acc[:], in0=wp[s][:], scalar=cact[:, kt:kt + 1], in1=acc[:], op0=ALU.mult, op1=ALU.add),
                    reads=[R_wp[s], R_cact] + R_acc, writes=R_acc)

        def mm_cols(e):
            ins = None
            for j in range(CPR):
                ins = e.matmul(banks[0][:, j:j + 1], lhsT=acc[:, j * 128:(j + 1) * 128], rhs=ones_f[:, 0:1],
                               start=True, stop=True)
            return ins
        S.op("tensor", mm_cols, reads=R_acc + [R_ones], writes=[RB[0]])
        S.op("vector", lambda e: e.tensor_tensor(out=modq[:], in0=banks[0][:, 0:CPR], in1=bcol[:], op=ALU.add),
             reads=[RB[0], R_c0], writes=[R_modq])
        S.dma("gpsimd", ccin_d, modq[:], reads=[R_modq], writes=[R_ccin], ch=1)
        S._deps("gpsimd", [R_ccin], [R_ccout])
        groups = [[g * NR + r for r in range(NR)] for g in range(c.NG)]
        S.cnt["cc"] += 1
        ccv = S.cnt["cc"]
        S.q["gpsimd"].append(lambda e: e.collective_compute(
            "AllGather", ALU.bypass, replica_groups=groups, ins=[ccin_d], outs=[ccout_d]).then_inc(S.sem["cc"], 1))
        S._mark(("cc", ccv), [R_ccin], [R_ccout])
        S.dma("sync", allcol[:].rearrange("p (r c) -> p r c", c=CPR), ccout_d.rearrange("(r p) c -> p r c", p=128),
              reads=[R_ccout], writes=[R_allcol], ch=5)
        S.op("vector", lambda e: e.tensor_copy(out=modcol[:], in_=allcol[:, 0:2 * KC]), reads=[R_allcol],
             writes=[R_modcol])
        S.op("vector", lambda e: e.scalar_tensor_tensor(out=Gcol[:], in0=allcol[:, KC:2 * KC], scalar=1.0,
                                                        in1=gaincol[:], op0=ALU.add, op1=ALU.mult),
             reads=[R_allcol, R_c0], writes=[R_Gcol])
        NJB = c.CGW // 128
        for j in range(KC):
            di = j % 2
            b = 1 + (j // NJB) % 2
            S.op("vector", lambda e, j=j, di=di: e.tensor_scalar(
                out=dg[di][:], in0=ident_f[:], scalar1=allcol[:, 2 * KC + j:2 * KC + j + 1], scalar2=None,
                op0=ALU.mult), reads=[R_identf, R_allcol], writes=[R_dg[di]])
            jj = j % NJB
            S.op("tensor", lambda e, b=b, di=di, jj=jj: e.matmul(
                banks[b][:, jj * 128:(jj + 1) * 128], lhsT=ones_f[:], rhs=dg[di][:], start=True, stop=True),
                reads=[R_dg[di], R_ones], writes=[RB[b]])
            if jj == NJB - 1:
                cg = j // NJB
                sl = slice(cg * c.CGW, (cg + 1) * c.CGW)
                S.op("vector", lambda e, b=b, sl=sl: e.tensor_copy(out=gate_sb[:, sl], in_=banks[b][:, 0:c.CGW]),
                     reads=[RB[b]], writes=[R_gate])
        S.dma("gpsimd", gscr_d, gate_sb[:], reads=[R_gate], writes=[R_gscr], ch=1)
        S.barrier()

        A = Region(nc, PH0, SB_END)
        NXB = 3
        xbuf = [A.alloc("xbuf%d" % i, [128, D], F32) for i in range(NXB)]
        junk = A.alloc("junk", [128, D], BF16)
        xs = [[A.alloc("xs%d_%d" % (a, b), [128, D], BF16) for b in range(4)] for a in range(2)]
        hst = [A.alloc("hst%d" % i, [128, KC * TGS], BF16) for i in range(2)]
        NTL = TT // 128
        ss = A.alloc("ss", [128, NTL], F32)
        rt = A.alloc("rt", [128, NTL], F32)
        rinv = A.alloc("rinv", [128, NTL], F32)
        R_xbuf = [Res("xbuf%d" % i) for i in range(NXB)]
        R_junk = Res("junk")
        R_xs = [[Res("xs") for _ in range(4)] for _ in range(2)]
        R_hst = [Res("hst0"), Res("hst1")]
        R_hstA = [Res("hstA0"), Res("hstA1")]
        R_ss = [Res("ss%d" % i) for i in range(NTL)]
        R_hscr = [Res("hscr%d" % i) for i in range(c.NTG)]
        CH_X = 1
        CH_HS = 2
        S.op("vector", lambda e: e.memset(ss[:], 0.0), writes=R_ss)
        def p1_prep(tg, tis=range(4)):
            for ti in tis:
                tl = tg * 4 + ti
                s = tl % NXB
                S.dma("sync", xbuf[s][:], x_d[tl * 128:(tl + 1) * 128, :], writes=[R_xbuf[s]], ch=CH_X + s)
                S.op("scalar", lambda e, s=s, tl=tl: e.activation(out=junk[:], in_=xbuf[s][:], func=AF.Square,
                                                                  accum_out=ss[:, tl:tl + 1]),
                     reads=[R_xbuf[s]], writes=[R_junk, R_ss[tl]])
                S.op("scalar", lambda e, tl=tl: e.activation(out=rt[:, tl:tl + 1], in_=ss[:, tl:tl + 1], func=AF.Sqrt,
                                                             bias=EPS, scale=1.0 / D),
                     reads=[R_ss[tl]], writes=[R_ss[tl]])
                S.op("vector", lambda e, tl=tl: e.reciprocal(out=rinv[:, tl:tl + 1], in_=rt[:, tl:tl + 1]),
                     reads=[R_ss[tl]], writes=[R_ss[tl]])
                xt = xs[tg % 2][ti]
                S.op("vector", lambda e, s=s, tl=tl, xt=xt: e.tensor_scalar(
                    out=xt[:], in0=xbuf[s][:], scalar1=rinv[:, tl:tl + 1], scalar2=None, op0=ALU.mult),
                    reads=[R_xbuf[s], R_ss[tl]], writes=[R_xs[tg % 2][ti]])

        def p1_tr(tg, js=None):
            hs = tg % 2
            for j in (range(KC) if js is None else js):
                b = j % 4

                def tr(e, b=b, j=j, tg=tg):
                    ins = None
                    for ti in range(4):
                        ins = e.matmul(banks[b][:, ti * 128:(ti + 1) * 128],
                                       lhsT=xs[tg % 2][ti][:, j * 128:(j + 1) * 128], rhs=ident_bf[:],
                                       start=True, stop=True)
                    return ins
                S.op("tensor", tr, reads=R_xs[tg % 2] + [R_ident], writes=[RB[b]])
                dst = hst[hs][:, j * TGS:(j + 1) * TGS]
                if j % 5 not in (0, 2):
                    S.op("vector", lambda e, b=b, j=j, dst=dst: e.tensor_scalar(
                        out=dst, in0=banks[b][:, 0:TGS], scalar1=Gcol[:, j:j + 1], scalar2=modcol[:, j:j + 1],
                        op0=ALU.mult, op1=ALU.add), reads=[RB[b], R_Gcol, R_modcol], writes=[R_hst[hs]])
                else:
                    S.op("scalar", lambda e, b=b, j=j, dst=dst: e.activation(
                        out=dst, in_=banks[b][:, 0:TGS], func=AF.Identity, bias=modcol[:, j:j + 1],
                        scale=Gcol[:, j:j + 1]), reads=[RB[b], R_Gcol, R_modcol], writes=[R_hstA[hs]])
            if js is None or js[-1] == KC - 1:
                S.dma("gpsimd", hscr_d[tg], hst[hs][:], reads=[R_hst[hs], R_hstA[hs]], writes=[R_hscr[tg]],
                      ch=CH_HS + hs)

        p1_prep(0)
        q4 = max(1, KC // 4)
        for tg in range(c.NTG):
            for ti in range(4):
                if tg + 1 < c.NTG:
                    p1_prep(tg + 1, [ti])
                js = list(range(ti * q4, KC if ti == 3 else (ti + 1) * q4))
                if js:
                    p1_tr(tg, js)
        S.barrier()

        C = Region(nc, PH0, SB_END)
        wbuf = [[C.alloc("wb%d_%d" % (a, b), [128, KC * 128], BF16) for b in range(4)] for a in range(2)]
        htile = [C.alloc("ht%d" % i, [128, KC * TGS], BF16) for i in range(2)]
        R_wbuf = [[Res("wb") for _ in range(4)] for _ in range(2)]
        R_ht = [Res("ht0"), Res("ht1")]
        PH2 = C.off
        CH_W = 4
        CH_HT = 6
        CH_YS = 12
        CH_WPL = 14
        CH_PY = 16
        ht_ctr = [0]
        pb_ctr = [0]
        R_yscr = [Res("yscr%d" % i) for i in range(c.NTGO)]

        def load_w(slot, cbs):
            for i, cb in enumerate(cbs):
                S.dma("gpsimd", wbuf[slot][i][:], win_d[cb], writes=[R_wbuf[slot][i]], ch=CH_W + slot * 4 + i)

        def load_ht(tg):
            hs = ht_ctr[0] % 2
            ht_ctr[0] += 1
            S.dma("sync", htile[hs][:], hscr_d[tg], reads=[R_hscr[tg]], writes=[R_ht[hs]], ch=CH_HT + hs)
            return hs

        def proj(slot, i, hs, ncols=TGS, col0=0):
            b = pb_ctr[0] % 3
            pb_ctr[0] += 1

            def fn(e):
                ins = None
                for k in range(KC):
                    ins = e.matmul(banks[b][:, 0:ncols], lhsT=wbuf[slot][i][:, k * 128:(k + 1) * 128],
                                   rhs=htile[hs][:, k * TGS + col0:k * TGS + col0 + ncols],
                                   start=(k == 0), stop=(k == KC - 1))
                return ins
            S.op("tensor", fn, reads=[R_wbuf[slot][i], R_ht[hs]], writes=[RB[b]])
            return b

        passes = []
        for h in range(NH):
            passes.append(("attn", h, [h, NH + h, 2 * NH + h, 3 * NH + h]))
        ub = 4 * NH
        gb = 4 * NH + c.DP // 128
        pool_passes = []
        for g in range(4):
            for c0 in range(0, CPG, 4):
                cc = list(range(c0, min(CPG, c0 + 4)))
                pool_passes.append(("pu", g, cc))
        for g in range(4):
            for c0 in range(0, CPG, 4):
                cc = list(range(c0, min(CPG, c0 + 4)))
                passes.append(("pu", (g, cc), [ub + g * CPG + x for x in cc]))
            for c0 in range(0, CPG, 4):
                cc = list(range(c0, min(CPG, c0 + 4)))
                passes.append(("pg", (g, cc), [gb + g * CPG + x for x in cc]))

        Aa = Region(nc, PH2, SB_END)
        KT = Aa.alloc("KT", [128, TT], BF16)
        QT = Aa.alloc("QT", [128, T], BF16)
        VT = Aa.alloc("VT", [128, TT], BF16)
        NBp = [T // (128 * d) for d in PATTERNS]
        vb_base = []
        nvb = 0
        for d, nb in zip(PATTERNS, NBp):
            vb_base.append(nvb)
            nvb += d * (nb + 1)
        NVBP = (nvb + 7) // 8 * 8
        Vb = Aa.alloc("Vb", [128, NVBP * 128], BF16)
        SG = Aa.alloc("SG", [128, T], BF16)
        oacc = Aa.alloc("oacc", [128, 2 * T], F32)
        tabs_sb = [Aa.alloc("tabs%d" % i, [128, 3 * 768], BF16) for i in range(2)]
        PTb = [Aa.alloc("PT%d" % i, [128, 256], BF16) for i in range(4)]
        yst = [Aa.alloc("yst0", [128, T], BF16)]
        yst.append(yst[0])
        R_KT, R_QT, R_VT, R_Vb, R_SG, R_oacc = Res("KT"), Res("QT"), Res("VT"), Res("Vb"), Res("SG"), Res("oacc")
        R_Vb2 = Res("Vb2")
        R_tabs = [Res("tabs0"), Res("tabs1")]
        R_PT = [Res("PT%d" % i) for i in range(4)]
        RS4 = [Res("S%d" % i) for i in range(4)]
        ROZ4 = [Res("OZ%d" % i) for i in range(4)]
        CH_TAB = 1
        R_yst = [Res("yst0")]
        R_yst.append(R_yst[0])
        SB_S = [3, 4, 0]
        SB_OZ = [5, 6, 1]
        SB_T = 7
        SC = 1.0 / math.sqrt(128.0)

        def sap(t, rowlen, start, step, count):
            return bass.AP(t, start, [[rowlen, 128], [step, count]])

        def attention_head(h, ys):
            ts = h % 2
            tb_ = tabs_sb[ts]
            vlist = []
            for pi, d in enumerate(PATTERNS):
                for r in range(d):
                    for n in range(-1, NBp[pi]):
                        vlist.append((HALO + 128 * n * d + r, d))
            for gi_, v0 in enumerate(range(0, len(vlist), 4)):
                grp = vlist[v0:v0 + 4]
                bT = (SB_T, 2, 5, 6)[gi_ % 4]

                def trv(e, grp=grp, bT=bT):
                    ins = None
                    for k, (start, d) in enumerate(grp):
                        ins = e.matmul(banks[bT][:, k * 128:(k + 1) * 128], lhsT=sap(VT, TT, start, d, 128),
                                       rhs=ident_bf[:], start=True, stop=True)
                    return ins
                S.op("tensor", trv, reads=[R_VT, R_ident], writes=[RB[bT]])
                n8 = len(grp)
                if gi_ % 2 == 0:
                    S.op("vector", lambda e, v0=v0, n8=n8, bT=bT: e.tensor_copy(
                        out=Vb[:, v0 * 128:(v0 + n8) * 128], in_=banks[bT][:, 0:n8 * 128]),
                        reads=[RB[bT]], writes=[R_Vb])
                else:
                    S.op("scalar", lambda e, v0=v0, n8=n8, bT=bT: e.activation(
                        out=Vb[:, v0 * 128:(v0 + n8) * 128], in_=banks[bT][:, 0:n8 * 128], func=AF.Copy),
                        reads=[RB[bT]], writes=[R_Vb2])
            blocks = []
            for pi, d in enumerate(PATTERNS):
                for r in range(d):
                    for n in range(NBp[pi]):
                        blocks.append((pi, d, r, n))

            def issue_S(bi):
                pi, d, r, n = blocks[bi]
                si = bi % 3
                bS, c0 = SB_S[si], 0
                kprev = sap(KT, TT, HALO + 128 * (n - 1) * d + r, d, 128)
                kcur = sap(KT, TT, HALO + 128 * n * d + r, d, 128)
                q = sap(QT, T, 128 * n * d + r, d, 128)
                t0 = pi * 768

                def mmS(e):
                    p0, p1 = ((t0 + 512, t0 + 640) if n == 0 else (t0, t0 + 256))
                    e.matmul(banks[bS][:, c0:c0 + 128], lhsT=kprev, rhs=q, start=True, stop=False)
                    e.matmul(banks[bS][:, c0:c0 + 128], lhsT=ident_bf[:], rhs=tb_[:, p0:p0 + 128],
                             start=False, stop=False)
                    e.matmul(banks[bS][:, c0:c0 + 128], lhsT=ident_bf[:], rhs=tb_[:, p1:p1 + 128],
                             start=False, stop=True)
                    e.matmul(banks[bS][:, c0 + 128:c0 + 256], lhsT=kcur, rhs=q, start=True, stop=False)
                    e.matmul(banks[bS][:, c0 + 128:c0 + 256], lhsT=ident_bf[:], rhs=tb_[:, t0 + 128:t0 + 256],
                             start=False, stop=False)
                    return e.matmul(banks[bS][:, c0 + 128:c0 + 256], lhsT=ident_bf[:],
                                    rhs=tb_[:, t0 + 384:t0 + 512], start=False, stop=True)
                S.op("tensor", mmS, reads=[R_KT, R_QT, R_tabs[ts], R_ident], writes=[RB[bS]])
                S.op("scalar", lambda e: e.activation(out=PTb[si][:], in_=banks[bS][:, c0:c0 + 256], func=AF.Exp,
                                                      scale=SC), reads=[RB[bS]], writes=[R_PT[si]])

            def issue_PV(bi):
                pi, d, r, n = blocks[bi]
                nb = NBp[pi]
                si = bi % 3
                bO, c0 = SB_OZ[si], 0
                vprev = vb_base[pi] + r * (nb + 1) + n
                vcur = vprev + 1

                def mmO(e):
                    e.matmul(banks[bO][:, c0:c0 + 128], lhsT=Vb[:, vprev * 128:(vprev + 1) * 128],
                             rhs=PTb[si][:, 0:128], start=True, stop=False)
                    e.matmul(banks[bO][:, c0:c0 + 128], lhsT=Vb[:, vcur * 128:(vcur + 1) * 128],
                             rhs=PTb[si][:, 128:256], start=False, stop=True)
                    e.matmul(banks[bO][:, c0 + 128:c0 + 256], lhsT=ones_bf[:], rhs=PTb[si][:, 0:128],
                             start=True, stop=False)
                    return e.matmul(banks[bO][:, c0 + 128:c0 + 256], lhsT=ones_bf[:], rhs=PTb[si][:, 128:256],
                                    start=False, stop=True)
                S.op("tensor", mmO, reads=[R_Vb, R_Vb2, R_PT[si], R_ones], writes=[RB[bO]])
                oap = bass.AP(oacc, 128 * n * d + r, [[2 * T, 128], [T, 2], [d, 128]])
                src = banks[bO][:, c0:c0 + 256].rearrange("p (a b) -> p a b", a=2)
                if pi == 0:
                    S.op("vector", lambda e: e.tensor_copy(out=oap, in_=src), reads=[RB[bO]], writes=[R_oacc])
                else:
                    S.op("vector", lambda e: e.tensor_tensor(out=oap, in0=oap, in1=src, op=ALU.add),
                         reads=[RB[bO], R_oacc], writes=[R_oacc])

            LOOK = 2
            for i in range(len(blocks) + LOOK):
                if i < len(blocks):
                    issue_S(i)
                if i - LOOK >= 0:
                    issue_PV(i - LOOK)
            S.op("vector", lambda e: e.reciprocal(out=oacc[:, T:2 * T], in_=oacc[:, T:2 * T]),
                 reads=[R_oacc], writes=[R_oacc])
            S.op("vector", lambda e: e.tensor_tensor(out=oacc[:, 0:T], in0=oacc[:, 0:T], in1=oacc[:, T:2 * T],
                                                     op=ALU.mult), reads=[R_oacc], writes=[R_oacc])
            S.op("vector", lambda e, ys=ys: e.tensor_tensor(out=yst[ys][:], in0=oacc[:, 0:T], in1=SG[:],
                                                            op=ALU.mult), reads=[R_oacc, R_SG], writes=[R_yst[ys]])
            ydst = yscr_d.rearrange("g p (m t) -> p g m t", t=TGS)[:, :, h, :]
            S.dma("gpsimd", ydst, yst[ys][:].rearrange("p (g t) -> p g t", t=TGS), reads=[R_yst[ys]],
                  writes=R_yscr, ch=CH_YS)

        def alloc_pool():
            Pp = Region(nc, PH2, SB_END)
            d = {}
            d["UT"] = [Pp.alloc("UT%d" % i, [128, 16 + T], F32) for i in range(CPG)]
            d["lv"] = [Pp.alloc("lv%d" % i, [128, 16 + T], F32) for i in range(2)]
            d["pooled"] = [Pp.alloc("pooled%d" % i, [128, T], BF16) for i in range(CPG)]
            d["sgp"] = [Pp.alloc("sgp%d" % i, [128, TGS], BF16) for i in range(3)]
            d["wpl"] = [Pp.alloc("wpl0", [128, CPG * CPG * 128], BF16)]
            d["wpl"].append(d["wpl"][0])
            d["pys"] = [Pp.alloc("pys%d" % i, [128, TGS], BF16) for i in range(2)]
            d["f16"] = Pp.alloc("f16", [128, 16], F32)
            return d

        npass = len(passes)
        E = Region(nc, PH0, SB_END)
        CGW, NCG, NT = c.CGW, c.NCG, c.NT
        wout, wout_off = [], []
        for i in range(2):
            wout.append(E.alloc("wout%d" % i, [128, MC * CGW], BF16))
            wout_off.append(E.last)
        ytile = [E.alloc("ytile%d" % i, [128, MC * TGS], BF16) for i in range(2)]
        R_wout = [Res("wout0"), Res("wout1")]
        R_yt = [Res("yt0"), Res("yt1")]
        CH_WO, CH_YT = 4, 1
        PREF3 = (npass % 2 == 0) and (MC * CGW <= 4 * KC * 128) and (MC == KC)
        load_w(0, passes[0][2])
        first_pool = True
        PB = None
        sgp_ctr = [0]
        for pidx, (kind, info, cbs) in enumerate(passes):
            slot = pidx % 2
            if pidx + 1 < npass:
                load_w(1 - slot, passes[pidx + 1][2])
            elif PREF3:
                S.dma("gpsimd", wout[0][:], wout_d[0], writes=[R_wout[0]] + R_wbuf[0], ch=CH_WO)
            if kind == "attn":
                h = info
                S.dma("sync", tabs_sb[h % 2][:], tabs_d[h], writes=[R_tabs[h % 2]], ch=CH_TAB + h % 2)
                tgs = list(range(c.NTG))
                hs_next = load_ht(tgs[0])
                for gi, tg in enumerate(tgs):
                    hs = hs_next
                    if gi + 1 < len(tgs):
                        hs_next = load_ht(tgs[gi + 1])
                    own = tg >= c.NTGH
                    t0 = tg * TGS
                    o0 = t0 - HALO
                    b = proj(slot, 1, hs)
                    S.op("scalar", lambda e, b=b, t0=t0: e.activation(out=KT[:, t0:t0 + TGS], in_=banks[b][:, 0:TGS],
                                                                      func=AF.Copy), reads=[RB[b]], writes=[R_KT])
                    b = proj(slot, 2, hs)
                    S.op("vector", lambda e, b=b, t0=t0: e.tensor_copy(out=VT[:, t0:t0 + TGS], in_=banks[b][:, 0:TGS]),
                         reads=[RB[b]], writes=[R_VT])
                    if own:
                        b = proj(slot, 0, hs)
                        S.op("vector", lambda e, b=b, o0=o0: e.tensor_copy(out=QT[:, o0:o0 + TGS],
                                                                           in_=banks[b][:, 0:TGS]),
                             reads=[RB[b]], writes=[R_QT])
                        b = proj(slot, 3, hs)
                        S.op("scalar", lambda e, b=b, o0=o0: e.activation(out=SG[:, o0:o0 + TGS],
                                                                          in_=banks[b][:, 0:TGS], func=AF.Silu),
                             reads=[RB[b]], writes=[R_SG])
                attention_head(h, h % 2)
            else:
                if first_pool:
                    S.barrier()
                    PB = alloc_pool()
                    R_UT = [Res("UT%d" % i) for i in range(CPG)]
                    R_lv = [Res("lv0"), Res("lv1")]
                    R_pooled = [Res("pooled%d" % i) for i in range(CPG)]
                    R_sgp = [Res("sgp%d" % i) for i in range(3)]
                    R_wpl = [Res("wpl0")]
                    R_wpl.append(R_wpl[0])
                    R_pys = [Res("pys%d" % i) for i in range(2)]
                    R_f16 = Res("f16")
                    first_pool = False
                g, ccs = info
                UT, lv, pooled, sgp, wpl, pys, f16 = (PB["UT"], PB["lv"], PB["pooled"], PB["sgp"], PB["wpl"],
                                                      PB["pys"], PB["f16"])
                if kind == "pu":
                    if ccs[0] == 0:
                        S.dma("gpsimd", wpl[g % 2][:], wpool_d[g], writes=[R_wpl[g % 2]], ch=CH_WPL)
                    tgs = [c.NTGH - 1] + list(range(c.NTGH, c.NTG))
                    hs_next = load_ht(tgs[0])
                    for gi, tg in enumerate(tgs):
                        hs = hs_next
                        if gi + 1 < len(tgs):
                            hs_next = load_ht(tgs[gi + 1])
                        for i, cc in enumerate(ccs):
                            if gi == 0:
                                b = proj(slot, i, hs, ncols=16, col0=TGS - 16)
                                S.op("vector", lambda e, b=b, cc=cc: e.tensor_scalar(
                                    out=UT[cc][:, 0:16], in0=banks[b][:, 0:16], scalar1=hflag[:, 0:1], scalar2=None,
                                    op0=ALU.mult), reads=[RB[b], R_const], writes=[R_UT[cc]])
                            else:
                                o0 = (tg - c.NTGH) * TGS
                                b = proj(slot, i, hs)
                                if (gi + i) % 2 == 0:
                                    S.op("vector", lambda e, b=b, cc=cc, o0=o0: e.tensor_copy(
                                        out=UT[cc][:, 16 + o0:16 + o0 + TGS], in_=banks[b][:, 0:TGS]),
                                        reads=[RB[b]], writes=[R_UT[cc]])
                                else:
                                    S.op("scalar", lambda e, b=b, cc=cc, o0=o0: e.activation(
                                        out=UT[cc][:, 16 + o0:16 + o0 + TGS], in_=banks[b][:, 0:TGS], func=AF.Copy),
                                        reads=[RB[b]], writes=[R_UT[cc]])
                    w = 2 ** (g + 1)
                    W_ = 16 + T
                    for cc in ccs:
                        src, rsrc = UT[cc], R_UT[cc]
                        cur, rcur = src, rsrc
                        for lvl in range(1, g + 2):
                            sh = 2 ** (lvl - 1)
                            v0 = 2 ** lvl - 1
                            dst, rdst = lv[(lvl - 1) % 2], R_lv[(lvl - 1) % 2]
                            S.op("vector", lambda e, dst=dst, cur=cur, v0=v0, sh=sh: e.tensor_tensor(
                                out=dst[:, v0:W_], in0=cur[:, v0:W_], in1=cur[:, v0 - sh:W_ - sh], op=ALU.add),
                                reads=[rcur], writes=[rdst])
                            cur, rcur = dst, rdst
                        S.op("vector", lambda e, cur=cur, src=src, cc=cc, w=w: e.scalar_tensor_tensor(
                            out=pooled[cc][:], in0=cur[:, 16:W_], scalar=1.0 / w, in1=src[:, 16:W_], op0=ALU.mult,
                            op1=ALU.subtract), reads=[rcur, rsrc], writes=[R_pooled[cc]])
                        S.op("vector", lambda e, cur=cur, g=g: e.tensor_tensor(
                            out=f16[:], in0=cur[:, 16:32], in1=invcnt[:, g * 16:(g + 1) * 16], op=ALU.mult),
                            reads=[rcur, R_const], writes=[R_f16])
                        S.op("vector", lambda e, src=src, cc=cc: e.tensor_tensor(
                            out=pooled[cc][:, 0:16], in0=f16[:], in1=src[:, 16:32], op=ALU.subtract),
                            reads=[R_f16, rsrc], writes=[R_pooled[cc]])
                else:
                    tgs = list(range(c.NTGH, c.NTG))
                    hs_next = load_ht(tgs[0])
                    for gi, tg in enumerate(tgs):
                        hs = hs_next
                        if gi + 1 < len(tgs):
                            hs_next = load_ht(tgs[gi + 1])
                        o0 = (tg - c.NTGH) * TGS
                        for i, dc in enumerate(ccs):
                            b = proj(slot, i, hs)
                            sg = sgp_ctr[0] % 3
                            py = sgp_ctr[0] % 2
                            sgp_ctr[0] += 1
                            S.op("scalar", lambda e, b=b, sg=sg: e.activation(out=sgp[sg][:], in_=banks[b][:, 0:TGS],
                                                                              func=AF.Silu),
                                 reads=[RB[b]], writes=[R_sgp[sg]])
                            b2 = pb_ctr[0] % 3
                            pb_ctr[0] += 1

                            def pmm(e, b2=b2, dc=dc, o0=o0, g=g):
                                ins = None
                                for cc in range(CPG):
                                    ins = e.matmul(banks[b2][:, 0:TGS],
                                                   lhsT=wpl[g % 2][:, (dc * CPG + cc) * 128:(dc * CPG + cc + 1) * 128],
                                                   rhs=pooled[cc][:, o0:o0 + TGS], start=(cc == 0),
                                                   stop=(cc == CPG - 1))
                                return ins
                            S.op("tensor", pmm, reads=R_pooled + [R_wpl[g % 2]], writes=[RB[b2]])
                            m = NH + g * CPG + dc
                            S.op("vector", lambda e, b2=b2, sg=sg, py=py, g=g, dc=dc: e.scalar_tensor_tensor(
                                out=pys[py][:], in0=banks[b2][:, 0:TGS], scalar=pscale[:, g * CPG + dc:g * CPG + dc + 1],
                                in1=sgp[sg][:], op0=ALU.mult, op1=ALU.mult),
                                reads=[RB[b2], R_sgp[sg], R_const], writes=[R_pys[py]])
                            tgo = tg - c.NTGH
                            S.dma("gpsimd", yscr_d[tgo][:, m * TGS:(m + 1) * TGS], pys[py][:], reads=[R_pys[py]],
                                  writes=[R_yscr[tgo]], ch=CH_PY + py)
        if PREF3:
            S.dma("sync", ytile[0][:], yscr_d[0], reads=[R_yscr[0]], writes=[R_yt[0], R_ht[0]], ch=CH_YT)
        S.barrier()

        gate_bc = E.alloc("gate_bc", [128, D], F32)
        fgain = E.alloc("fgain", [128, D], F32)
        NXT = 6
        xt3 = [E.alloc("xt%d" % i, [128, CGW], F32) for i in range(NXT)]
        xo3 = [E.alloc("xo%d" % i, [128, CGW], F32) for i in range(NXT)]
        junk3 = E.alloc("junk3", [128, CGW], BF16)
        ssq = E.alloc("ssq", [128, NT * NCG], F32)
        ssum = E.alloc("ssum", [128, NT], F32)
        rr = E.alloc("rr", [128, NT], F32)
        wfree = 1 - (NCG - 1) % 2
        assert NCG >= 2 and MC * CGW * 2 >= 2 * D * 4
        xr = [E.alloc("xr%d" % i, [128, D], F32, at=wout_off[wfree] + i * D * 4) for i in range(2)]
        xr.append(E.alloc("xr2", [128, D], F32))
        R_xr = [Res("xr0"), Res("xr1"), Res("xr2")]
        xr_used = [False, False, True]
        R_ssqt = [Res("ssq%d" % i) for i in range(NT)]
        R_g3 = Res("g3")
        R_xt = [Res("xt%d" % i) for i in range(NXT)]
        R_xo = [Res("xo%d" % i) for i in range(NXT)]
        R_junk3 = Res("junk3")
        R_out = [Res("out%d" % i) for i in range(NT)]
        CH_XO, CH_FS = 6, 12
        CH_XT, CH_XR, CH_G3 = 3, 9, 0
        S.dma("sync", gate_bc[:], gscr_d, reads=[R_gscr], writes=[R_g3], ch=CH_G3)
        S.dma("sync", fgain[:], fgain_d, writes=[R_g3], ch=CH_G3)
        S.op("vector", lambda e: e.memset(ssq[:], 0.0), writes=R_ssqt)
        it = 0
        rot = 0
        if not PREF3:
            S.dma("gpsimd", wout[0][:], wout_d[0], writes=[R_wout[0]], ch=CH_WO)

        def final_a(tl):
            s = tl % 3
            S.op("vector", lambda e: e.tensor_reduce(out=ssum[:, tl:tl + 1], in_=ssq[:, tl * NCG:(tl + 1) * NCG],
                                                     axis=AX.X, op=ALU.add), reads=[R_ssqt[tl]], writes=[R_ssqt[tl]])
            S.op("scalar", lambda e: e.activation(out=rr[:, tl:tl + 1], in_=ssum[:, tl:tl + 1], func=AF.Sqrt,
                                                  bias=EPS, scale=1.0 / D), reads=[R_ssqt[tl]], writes=[R_ssqt[tl]])
            S.op("vector", lambda e: e.reciprocal(out=rr[:, tl:tl + 1], in_=rr[:, tl:tl + 1]),
                 reads=[R_ssqt[tl]], writes=[R_ssqt[tl]])
            wr = [R_xr[s]] + ([] if xr_used[s] else [R_wout[wfree]])
            xr_used[s] = True
            S.dma("scalar", xr[s][:], out_d[tl * 128:(tl + 1) * 128, :], reads=[R_out[tl]], writes=wr, ch=CH_XR + s)

        def final_b(tl):
            s = tl % 3
            S.op("vector", lambda e: e.scalar_tensor_tensor(
                out=xr[s][:], in0=xr[s][:], scalar=rr[:, tl:tl + 1], in1=fgain[:], op0=ALU.mult, op1=ALU.mult),
                reads=[R_xr[s], R_ssqt[tl], R_g3], writes=[R_xr[s]])
            S.dma("gpsimd", out_d[tl * 128:(tl + 1) * 128, :], xr[s][:], reads=[R_xr[s]], writes=[R_out[tl]],
                  ch=CH_FS + s)
        for cg in range(NCG):
            ws = cg % 2
            if cg + 1 < NCG:
                S.dma("gpsimd", wout[1 - ws][:], wout_d[cg + 1], writes=[R_wout[1 - ws]], ch=CH_WO + 1 - ws)
            for tg in range(c.NTGO):
                ysl = it % 2
                if it == 0 and not PREF3:
                    S.dma("sync", ytile[0][:], yscr_d[0], reads=[R_yscr[0]], writes=[R_yt[0]], ch=CH_YT)
                it += 1
                if it < NCG * c.NTGO:
                    ntg = it % c.NTGO
                    S.dma("sync", ytile[1 - ysl][:], yscr_d[ntg], reads=[R_yscr[ntg]], writes=[R_yt[1 - ysl]],
                          ch=CH_YT + 1 - ysl)
                for ti in range(4):
                    tl = tg * 4 + ti
                    ro = rot % NXT
                    rot += 1
                    S.dma("sync", xt3[ro][:], x_d[HALO + tl * 128:HALO + (tl + 1) * 128, cg * CGW:(cg + 1) * CGW],
                          writes=[R_xt[ro]], ch=CH_XT + ro)
                    b = pb_ctr[0] % 3
                    pb_ctr[0] += 1

                    def omm(e, b=b, ysl=ysl, ws=ws, ti=ti):
                        ins = None
                        for m in range(MC):
                            ins = e.matmul(banks[b][:, 0:CGW],
                                           lhsT=ytile[ysl][:, m * TGS + ti * 128:m * TGS + (ti + 1) * 128],
                                           rhs=wout[ws][:, m * CGW:(m + 1) * CGW], start=(m == 0), stop=(m == MC - 1))
                        return ins
                    S.op("tensor", omm, reads=[R_yt[ysl], R_wout[ws]], writes=[RB[b]])
                    gsl = slice(cg * CGW, (cg + 1) * CGW)
                    S.op("vector", lambda e, b=b, ro=ro, gsl=gsl: e.tensor_tensor(
                        out=xo3[ro][:], in0=banks[b][:, 0:CGW], in1=gate_bc[:, gsl], op=ALU.mult),
                        reads=[RB[b], R_g3], writes=[R_xo[ro]])
                    S.op("vector", lambda e, ro=ro: e.tensor_tensor(out=xo3[ro][:], in0=xo3[ro][:], in1=xt3[ro][:],
                                                                    op=ALU.add),
                         reads=[R_xo[ro], R_xt[ro]], writes=[R_xo[ro]])
                    col = tl * NCG + cg
                    S.op("scalar", lambda e, ro=ro, col=col: e.activation(out=junk3[:], in_=xo3[ro][:], func=AF.Square,
                                                                          accum_out=ssq[:, col:col + 1]),
                         reads=[R_xo[ro], R_ssqt[tl]], writes=[R_junk3, R_ssqt[tl]])
                    S.dma("gpsimd", out_d[tl * 128:(tl + 1) * 128, cg * CGW:(cg + 1) * CGW], xo3[ro][:],
                          reads=[R_xo[ro]], writes=[R_out[tl]], ch=CH_XO + ro)
                    if cg == NCG - 1:
                        if tl >= 1:
                            final_a(tl - 1)
                        if tl >= 2:
                            final_b(tl - 2)
        final_a(NT - 1)
        final_b(NT - 2)
        final_b(NT - 1)
        S.final_wait("sync", R_out)
        S.final_wait("gpsimd", R_out)
        S.emit(block)
    return nc


def make_tables():
    kp = np.arange(128)[:, None]
    qi = np.arange(128)[None, :]
    rel0 = 128 + qi - kp
    rel1 = qi - kp
    relT = np.concatenate([rel0, rel1], axis=1).astype(np.float32)
    valid = np.concatenate([(rel0 >= 0) & (rel0 <= 128), (rel1 >= 0) & (rel1 <= 128)], axis=1)
    maskT = np.where(valid, 0.0, NEG).astype(np.float32)
    relT = np.where(valid, relT, 0.0).astype(np.float32)
    return relT, maskT


def make_tabs(NH, first):
    import ml_dtypes
    relT, maskT = make_tables()
    SC = 1.0 / math.sqrt(128.0)
    out = np.zeros((NH, 128, 3 * 768), dtype=ml_dtypes.bfloat16)
    hm = np.zeros((128, 256), np.float32)
    if first:
        hm[:, 0:128] = NEG

    def split(v):
        hi = v.astype(ml_dtypes.bfloat16)
        lo = (v - hi.astype(np.float32)).astype(ml_dtypes.bfloat16)
        return hi, lo
    for h in range(NH):
        slope = 2.0 ** (-8.0 * (h + 1) / NH)
        for pi, d in enumerate(PATTERNS):
            v = ((relT * np.float32(-slope * d) + maskT) / np.float32(SC)).astype(np.float32)
            vf = ((relT * np.float32(-slope * d) + maskT + hm) / np.float32(SC)).astype(np.float32)
            hi, lo = split(v)
            fhi, flo = split(vf)
            o = pi * 768
            out[h, :, o:o + 256] = hi
            out[h, :, o + 256:o + 512] = lo
            out[h, :, o + 512:o + 640] = fhi[:, 0:128]
            out[h, :, o + 640:o + 768] = flo[:, 0:128]
    return out


def col_layout(v):
    v = np.asarray(v, np.float32)
    return np.ascontiguousarray(v.reshape(-1, 128).T)


def prep_inputs(cfg, x, c, norm_gain, w_ada, b_ada, w_in, w_pool, pool_scale, w_out, final_gain, seq_chunks):
    cf = cfg
    D, KC, T, HALO, MC, CPG = cf.D, cf.KC, cf.T, cf.HALO, cf.MC, cf.CPG
    B = x.shape[0]
    relT, maskT = make_tables()
    w_ada0 = np.asarray(w_ada[0], np.float32)
    b = np.asarray(b_ada[0], np.float32)
    CWQ = cf.CPR * 128
    wq = [np.ascontiguousarray(w_ada0[:, r * CWQ:(r + 1) * CWQ]) for r in range(seq_chunks)]
    bq = [col_layout(b[r * CWQ:(r + 1) * CWQ]) for r in range(seq_chunks)]
    shared = {
        "gain_col": col_layout(norm_gain[0]),
        "w_in_r": np.ascontiguousarray(
            np.asarray(w_in[0], np.float32).reshape(KC, 128, cf.NCB, 128).transpose(2, 1, 0, 3)).reshape(
                cf.NCB, 128, KC * 128),
        "w_pool_r": np.ascontiguousarray(
            np.asarray(w_pool[0], np.float32).reshape(4, CPG, 128, CPG, 128).transpose(0, 2, 3, 1, 4)).reshape(
                4, 128, CPG * CPG * 128),
        "pscale_col": col_layout(pool_scale[0]),
        "w_out_r": np.ascontiguousarray(
            np.asarray(w_out[0], np.float32).reshape(MC, 128, cf.NCG, cf.CGW).transpose(2, 1, 0, 3)).reshape(
                cf.NCG, 128, MC * cf.CGW),
        "fgain_bc": np.ascontiguousarray(np.broadcast_to(np.asarray(final_gain, np.float32)[None, :], (128, D))),
        "relT": relT, "maskT": maskT, "ident": np.eye(128, dtype=np.float32),
    }
    in_maps = []
    tabs_first = make_tabs(cf.NH, True)
    tabs_rest = make_tabs(cf.NH, False)
    for bi in range(B):
        cT = col_layout(c[bi])
        for ci in range(seq_chunks):
            first = ci == 0
            xo = np.asarray(x[bi, ci * T:(ci + 1) * T], np.float32)
            if first:
                xh = np.zeros((HALO, D), np.float32)
            else:
                xh = np.asarray(x[bi, ci * T - HALO:ci * T], np.float32)
            hflag = np.empty((128, 2), np.float32)
            hflag[:, 0] = 0.0 if first else 1.0
            hflag[:, 1] = NEG if first else 0.0
            inv = np.empty((4, 16), np.float32)
            for g in range(4):
                w = 2 ** (g + 1)
                if first:
                    inv[g] = 1.0 / np.minimum(np.arange(1, 17), w)
                else:
                    inv[g] = 1.0 / w
            m = dict(shared)
            m["x"] = np.concatenate([xh, xo], axis=0)
            m["cT"] = cT
            m["hflag"] = hflag
            m["w_ada_q"] = wq[ci]
            m["b_ada_col"] = bq[ci]
            m["tabs"] = tabs_first if first else tabs_rest
            m["invcnt"] = np.ascontiguousarray(np.broadcast_to(inv.reshape(1, 64), (128, 64)))
            in_maps.append(m)
    return in_maps


_NC_CACHE = {}


def kernel(x, c, norm_gain, w_ada, b_ada, w_in, w_pool, pool_scale, w_out, final_gain):
    x = np.asarray(x)
    B, SEQ, D = x.shape
    cfg = Cfg(NR=SEQ // 2048, NG=B)
    nchunk = SEQ // cfg.T
    in_maps = prep_inputs(cfg, x, np.asarray(c), np.asarray(norm_gain), np.asarray(w_ada), np.asarray(b_ada),
                          np.asarray(w_in), np.asarray(w_pool), np.asarray(pool_scale), np.asarray(w_out),
                          np.asarray(final_gain), nchunk)
    if "nc" not in _NC_CACHE:
        _NC_CACHE["nc"] = build_nc(cfg)
    nc = _NC_CACHE["nc"]
    n = len(in_maps)
    res = run_bass_kernel_spmd(nc, in_maps, core_ids=list(range(n)))
    out = np.empty((B, SEQ, D), np.float32)
    k = 0
    for bi in range(B):
        for ci in range(nchunk):
            out[bi, ci * cfg.T:(ci + 1) * cfg.T] = np.asarray(res.results[k]["out"], np.float32)
            k += 1
    return out
```

```python
import math
import numpy as np
import concourse.bass as bass
import concourse.mybir as mybir
from concourse.bass_utils import run_bass_kernel_spmd

F32 = mybir.dt.float32
BF16 = mybir.dt.bfloat16
ALU = mybir.AluOpType
AF = mybir.ActivationFunctionType
AX = mybir.AxisListType

ENGS = ("tensor", "vector", "scalar", "gpsimd", "sync")
EPS = 1e-6
NEG = -30000.0
PATTERNS = (1, 4, 16)


class Res:
    __slots__ = ("name", "w", "r")

    def __init__(self, name):
        self.name = name
        self.w = None
        self.r = {}


class Sched:
    def __init__(self, eng_sems, dma_sems, gdma_sems):
        self.q = {e: [] for e in ENGS}
        self.sem = dict(eng_sems)
        for i, s in enumerate(dma_sems):
            self.sem[("dma", i)] = s
        for i, s in enumerate(gdma_sems):
            self.sem[("gdma", i)] = s
        self.cnt = {k: 0 for k in self.sem}
        self.seen = {e: {} for e in ENGS}
        self.nwait = 0

    def _wait(self, eng, k, v):
        seen = self.seen[eng]
        if seen.get(k, 0) < v:
            seen[k] = v
            h = self.sem[k]
            self.q[eng].append(lambda e, h=h, v=v: e.wait_ge(h, v))
            self.nwait += 1

    def _deps(self, eng, reads, writes):
        need = {}
        for r in reads:
            if r.w is not None:
                k, v = r.w
                if need.get(k, 0) < v:
                    need[k] = v
        for w in writes:
            if w.w is not None:
                k, v = w.w
                if need.get(k, 0) < v:
                    need[k] = v
            for k, v in w.r.items():
                if need.get(k, 0) < v:
                    need[k] = v
        for k, v in need.items():
            self._wait(eng, k, v)

    def _mark(self, tick, reads, writes):
        k, v = tick
        for r in reads:
            if r.r.get(k, 0) < v:
                r.r[k] = v
        for w in writes:
            w.w = tick
            w.r = {}

    def op(self, eng, fn, reads=(), writes=()):
        self._deps(eng, reads, writes)
        self.cnt[eng] += 1
        v = self.cnt[eng]
        h = self.sem[eng]
        self.q[eng].append(lambda e, fn=fn, h=h: fn(e).then_inc(h, 1))
        self._mark((eng, v), reads, writes)

    def dma(self, eng, out, in_, reads=(), writes=(), ch=0):
        self._deps(eng, reads, writes)
        k = ("gdma" if eng == "gpsimd" else "dma", ch)
        self.cnt[k] += 16
        v = self.cnt[k]
        h = self.sem[k]
        self.q[eng].append(lambda e, out=out, in_=in_, h=h: e.dma_start(out=out, in_=in_).then_inc(h, 16))
        self._mark((k, v), reads, writes)

    def barrier(self):
        for eng in ENGS:
            for k, v in self.cnt.items():
                if v > 0:
                    self._wait(eng, k, v)

    def final_wait(self, eng, ress):
        self._deps(eng, ress, ())

    def emit(self, block):
        for name in ENGS:
            lst = self.q[name]

            def body(e, lst=lst):
                for th in lst:
                    th(e)
            getattr(block, name)(body)


class Region:
    def __init__(self, nc, start, end):
        self.nc, self.off, self.end = nc, start, end
        self.n = 0

    def alloc(self, name, shape, dt, at=None):
        esz = 4 if dt == F32 else 2
        nbytes = int(np.prod(shape[1:])) * esz
        if at is not None:
            Region.uid = getattr(Region, "uid", 0) + 1
            return self.nc.alloc_sbuf_tensor_at("%s_%d" % (name, Region.uid), list(shape), dt, offset=at)
        off = (self.off + 31) // 32 * 32
        self.last = off
        assert off + nbytes <= self.end, (name, off, nbytes, self.end)
        Region.uid = getattr(Region, "uid", 0) + 1
        t = self.nc.alloc_sbuf_tensor_at("%s_%d" % (name, Region.uid), list(shape), dt, offset=off)
        self.off = off + nbytes
        return t


class Cfg:
    def __init__(self, D=4096, NH=16, GP=512, T=2048, HALO=2048, NPAIR=4):
        self.D, self.NH, self.GP, self.T, self.HALO = D, NH, GP, T, HALO
        self.NR, self.NPAIR = 2, NPAIR
        self.groups = [[p, p + NPAIR] for p in range(NPAIR)]
        self.CPR = 3 * (D // 128) // 2
        self.KC = D // 128
        self.DA = NH * 128
        self.DP = 4 * GP
        self.CPG = GP // 128
        self.NIN = 4 * self.DA + 2 * self.DP
        self.NCB = self.NIN // 128
        self.DMIX = self.DA + self.DP
        self.MC = self.DMIX // 128
        self.TGS = 512
        self.NTGO = T // self.TGS
        self.NTGH = HALO // self.TGS
        self.NTG = self.NTGO + self.NTGH
        self.TT = HALO + T
        self.D3 = 3 * D
        self.CW = 2048 if self.D3 % 2048 == 0 else 1024
        self.NPIECE = self.D3 // self.CW
        self.CGW = min(512, D)
        self.NCG = D // self.CGW
        self.NT = T // 128


def build_nc(cfg):
    c = cfg
    D, NH, KC, T, HALO, TT, TGS, MC, CPG = c.D, c.NH, c.KC, c.T, c.HALO, c.TT, c.TGS, c.MC, c.CPG
    nc = bass.Bass("TRN2", target_bir_lowering=False)
    dr = lambda name, shape, dt=F32, kind="ExternalInput": nc.dram_tensor(name, list(shape), dt, kind=kind).ap()
    x_d = dr("x", [TT, D])
    cT_d = dr("cT", [128, KC])
    CPR, NR = c.CPR, c.NR
    wada_d = dr("w_ada_q", [D, CPR * 128])
    bcol_d = dr("b_ada_col", [128, CPR])
    ccin_d = dr("cc_in", [128, CPR], F32, "Internal")
    ccout_d = dr("cc_out", [NR * 128, CPR], F32, "Internal")
    gaincol_d = dr("gain_col", [128, KC])
    win_d = dr("w_in_r", [c.NCB, 128, KC * 128])
    wpool_d = dr("w_pool_r", [4, 128, CPG * CPG * 128])
    pscale_d = dr("pscale_col", [128, c.DP // 128])
    wout_d = dr("w_out_r", [c.NCG, 128, MC * c.CGW])
    fgain_d = dr("fgain_bc", [128, D])
    hflag_d = dr("hflag", [128, 2])
    invcnt_d = dr("invcnt", [128, 64])
    relT_d = dr("relT", [128, 256])
    maskT_d = dr("maskT", [128, 256])
    ident_d = dr("ident", [128, 128])
    tabs_d = dr("tabs", [NH, 128, 3 * 768], BF16)
    out_d = dr("out", [T, D], F32, "ExternalOutput")
    hscr_d = dr("hT_scr", [c.NTG, 128, KC * TGS], BF16, "Internal")
    yscr_d = dr("yT_scr", [c.NTGO, 128, MC * TGS], BF16, "Internal")
    gscr_d = dr("gate_scr", [128, D], F32, "Internal")

    SB_END = 224 * 1024
    banks = [nc.alloc_psum_tensor("bank%d" % i, [128, 512], F32) for i in range(8)]
    RB = [Res("bank%d" % i) for i in range(8)]

    def bank_bf(i):
        return banks[i][:].bitcast(BF16)

    from contextlib import ExitStack
    with ExitStack() as st:
        eng_sems = {e: st.enter_context(nc.semaphore("s_" + e)) for e in ENGS}
        dma_sems = [st.enter_context(nc.semaphore("d%d" % i)) for i in range(12)]
        gdma_sems = [st.enter_context(nc.semaphore("g%d" % i)) for i in range(20)]
        cc_sem = st.enter_context(nc.semaphore("ccsem"))
        block = st.enter_context(nc.Block())
        S = Sched(eng_sems, dma_sems, gdma_sems)
        S.sem["cc"] = cc_sem
        S.cnt["cc"] = 0

        SB_BASE = 17408
        P = Region(nc, SB_BASE, SB_BASE + 8 * 1024)
        ident_bf = P.alloc("ident", [128, 128], BF16)
        ones_f = P.alloc("ones_f", [128, 128], F32)
        ident_f = P.alloc("ident_f", [128, 128], F32)
        ones_bf = P.alloc("ones_bf", [128, 128], BF16)
        modcol = P.alloc("modcol", [128, 2 * KC], F32)
        Gcol = P.alloc("Gcol", [128, KC], F32)
        hflag = P.alloc("hflag", [128, 2], F32)
        invcnt = P.alloc("invcnt", [128, 64], F32)
        relT = P.alloc("relT", [128, 256], F32)
        maskT = P.alloc("maskT", [128, 256], F32)
        hmfull = P.alloc("hmfull", [128, 256], F32)
        pscale = P.alloc("pscale", [128, c.DP // 128], F32)
        PH0 = P.off + 32
        R_const = Res("const")
        R_ident, R_ones, R_modcol, R_Gcol, R_hm = Res("ident"), Res("ones"), Res("modcol"), Res("Gcol"), Res("hm")

        CH_CONST = 0
        for dst, src in ((hflag, hflag_d), (invcnt, invcnt_d), (relT, relT_d), (maskT, maskT_d), (pscale, pscale_d)):
            S.dma("sync", dst[:], src, writes=[R_const], ch=CH_CONST)
        S.dma("gpsimd", ident_bf[:], ident_d, writes=[R_ident], ch=0)
        R_identf = Res("identf")
        S.dma("sync", ident_f[:], ident_d, writes=[R_identf], ch=6)
        S.op("vector", lambda e: e.memset(ones_f[:], 1.0), writes=[R_ones])
        S.op("vector", lambda e: e.memset(ones_bf[:], 1.0), writes=[R_ones])
        S.op("vector", lambda e: e.memset(hmfull[:], 0.0), writes=[R_hm])
        S.op("vector", lambda e: e.tensor_scalar(out=hmfull[:, 0:128], in0=hmfull[:, 0:128], scalar1=hflag[:, 1:2],
                                                 scalar2=None, op0=ALU.add), reads=[R_const], writes=[R_hm])

        A = Region(nc, PH0, SB_END)
        CWQ = CPR * 128
        acc = A.alloc("acc", [128, CWQ], F32)
        NWP = 4
        wp = [A.alloc("wp%d" % i, [128, CWQ], F32) for i in range(NWP)]
        cT = A.alloc("cT", [128, KC], F32)
        cact = A.alloc("cact", [128, KC], F32)
        bcol = A.alloc("bcol", [128, CPR], F32)
        modq = A.alloc("modq", [128, CPR], F32)
        allcol = A.alloc("allcol", [128, NR * CPR], F32)
        gaincol = A.alloc("gaincol", [128, KC], F32)
        gate_sb = A.alloc("gate_sb", [128, D], F32)
        dg = [A.alloc("dg%d" % i, [128, 128], F32) for i in range(2)]
        R_wp = [Res("wp%d" % i) for i in range(NWP)]
        R_acc = [Res("acc")]
        R_c0, R_cact, R_gate = Res("c0"), Res("cact"), Res("gate_sb")
        R_gscr, R_modq, R_ccin, R_ccout, R_allcol = Res("gscr"), Res("modq"), Res("ccin"), Res("ccout"), Res("allcol")
        R_dg = [Res("dg0"), Res("dg1")]
        CH_WP = 1
        for dst, src in ((cT, cT_d), (bcol, bcol_d), (gaincol, gaincol_d)):
            S.dma("sync", dst[:], src, writes=[R_c0], ch=5)
        S.op("scalar", lambda e: e.activation(out=cact[:], in_=cT[:], func=AF.Silu), reads=[R_c0], writes=[R_cact])
        for kt in range(KC):
            s = kt % NWP
            S.dma("sync", wp[s][:], wada_d[kt * 128:(kt + 1) * 128, :], writes=[R_wp[s]], ch=CH_WP + s)
            if kt == 0:
                S.op("vector", lambda e, s=s: e.tensor_scalar(
                    out=acc[:], in0=wp[s][:], scalar1=cact[:, 0:1], scalar2=None, op0=ALU.mult),
                    reads=[R_wp[s], R_cact], writes=R_acc)
            else:
                S.op("vector", lambda e, s=s, kt=kt: e.scalar_tensor_tensor(
                    out=acc[:], in0=wp[s][:], scalar=cact[:, kt:kt + 1], in1=acc[:], op0=ALU.mult, op1=ALU.add),
                    reads=[R_wp[s], R_cact] + R_acc, writes=R_acc)

        def mm_cols(e):
            ins = None
            for j in range(CPR):
                ins = e.matmul(banks[0][:, j:j + 1], lhsT=acc[:, j * 128:(j + 1) * 128], rhs=ones_f[:, 0:1],
                               start=True, stop=True)
            return ins
        S.op("tensor", mm_cols, reads=R_acc + [R_ones], writes=[RB[0]])
        S.op("vector", lambda e: e.tensor_tensor(out=modq[:], in0=banks[0][:, 0:CPR], in1=bcol[:], op=ALU.add),
             reads=[RB[0], R_c0], writes=[R_modq])
        S.dma("gpsimd", ccin_d, modq[:], reads=[R_modq], writes=[R_ccin], ch=1)
        S._deps("gpsimd", [R_ccin], [R_ccout])
        groups = c.groups
        S.cnt["cc"] += 1
        ccv = S.cnt["cc"]
        S.q["gpsimd"].append(lambda e: e.collective_compute(
            "AllGather", ALU.bypass, replica_groups=groups, ins=[ccin_d], outs=[ccout_d]).then_inc(S.sem["cc"], 1))
        S._mark(("cc", ccv), [R_ccin], [R_ccout])
        S.dma("sync", allcol[:].rearrange("p (r c) -> p r c", c=CPR), ccout_d.rearrange("(r p) c -> p r c", p=128),
              reads=[R_ccout], writes=[R_allcol], ch=5)
        S.op("vector", lambda e: e.tensor_copy(out=modcol[:], in_=allcol[:, 0:2 * KC]), reads=[R_allcol],
             writes=[R_modcol])
        S.op("vector", lambda e: e.scalar_tensor_tensor(out=Gcol[:], in0=allcol[:, KC:2 * KC], scalar=1.0,
                                                        in1=gaincol[:], op0=ALU.add, op1=ALU.mult),
             reads=[R_allcol, R_c0], writes=[R_Gcol])
        NJB = c.CGW // 128
        for j in range(KC):
            di = j % 2
            b = 1 + (j // NJB) % 2
            S.op("vector", lambda e, j=j, di=di: e.tensor_scalar(
                out=dg[di][:], in0=ident_f[:], scalar1=allcol[:, 2 * KC + j:2 * KC + j + 1], scalar2=None,
                op0=ALU.mult), reads=[R_identf, R_allcol], writes=[R_dg[di]])
            jj = j % NJB
            S.op("tensor", lambda e, b=b, di=di, jj=jj: e.matmul(
                banks[b][:, jj * 128:(jj + 1) * 128], lhsT=ones_f[:], rhs=dg[di][:], start=True, stop=True),
                reads=[R_dg[di], R_ones], writes=[RB[b]])
            if jj == NJB - 1:
                cg = j // NJB
                sl = slice(cg * c.CGW, (cg + 1) * c.CGW)
                S.op("vector", lambda e, b=b, sl=sl: e.tensor_copy(out=gate_sb[:, sl], in_=banks[b][:, 0:c.CGW]),
                     reads=[RB[b]], writes=[R_gate])
        S.dma("gpsimd", gscr_d, gate_sb[:], reads=[R_gate], writes=[R_gscr], ch=1)
        S.barrier()

        A = Region(nc, PH0, SB_END)
        NXB = 3
        xbuf = [A.alloc("xbuf%d" % i, [128, D], F32) for i in range(NXB)]
        junk = A.alloc("junk", [128, D], BF16)
        xs = [[A.alloc("xs%d_%d" % (a, b), [128, D], BF16) for b in range(4)] for a in range(2)]
        hst = [A.alloc("hst%d" % i, [128, KC * TGS], BF16) for i in range(2)]
        NTL = TT // 128
        ss = A.alloc("ss", [128, NTL], F32)
        rt = A.alloc("rt", [128, NTL], F32)
        rinv = A.alloc("rinv", [128, NTL], F32)
        R_xbuf = [Res("xbuf%d" % i) for i in range(NXB)]
        R_junk = Res("junk")
        R_xs = [[Res("xs") for _ in range(4)] for _ in range(2)]
        R_hst = [Res("hst0"), Res("hst1")]
        R_hstA = [Res("hstA0"), Res("hstA1")]
        R_ss = [Res("ss%d" % i) for i in range(NTL)]
        R_hscr = [Res("hscr%d" % i) for i in range(c.NTG)]
        CH_X = 1
        CH_HS = 2
        S.op("vector", lambda e: e.memset(ss[:], 0.0), writes=R_ss)
        def p1_prep(tg, tis=range(4)):
            for ti in tis:
                tl = tg * 4 + ti
                s = tl % NXB
                S.dma("sync", xbuf[s][:], x_d[tl * 128:(tl + 1) * 128, :], writes=[R_xbuf[s]], ch=CH_X + s)
                S.op("scalar", lambda e, s=s, tl=tl: e.activation(out=junk[:], in_=xbuf[s][:], func=AF.Square,
                                                                  accum_out=ss[:, tl:tl + 1]),
                     reads=[R_xbuf[s]], writes=[R_junk, R_ss[tl]])
                S.op("scalar", lambda e, tl=tl: e.activation(out=rt[:, tl:tl + 1], in_=ss[:, tl:tl + 1], func=AF.Sqrt,
                                                             bias=EPS, scale=1.0 / D),
                     reads=[R_ss[tl]], writes=[R_ss[tl]])
                S.op("vector", lambda e, tl=tl: e.reciprocal(out=rinv[:, tl:tl + 1], in_=rt[:, tl:tl + 1]),
                     reads=[R_ss[tl]], writes=[R_ss[tl]])
                xt = xs[tg % 2][ti]
                S.op("vector", lambda e, s=s, tl=tl, xt=xt: e.tensor_scalar(
                    out=xt[:], in0=xbuf[s][:], scalar1=rinv[:, tl:tl + 1], scalar2=None, op0=ALU.mult),
                    reads=[R_xbuf[s], R_ss[tl]], writes=[R_xs[tg % 2][ti]])

        def p1_tr(tg, js=None):
            hs = tg % 2
            for j in (range(KC) if js is None else js):
                b = j % 4

                def tr(e, b=b, j=j, tg=tg):
                    ins = None
                    for ti in range(4):
                        ins = e.matmul(banks[b][:, ti * 128:(ti + 1) * 128],
                                       lhsT=xs[tg % 2][ti][:, j * 128:(j + 1) * 128], rhs=ident_bf[:],
                                       start=True, stop=True)
                    return ins
                S.op("tensor", tr, reads=R_xs[tg % 2] + [R_ident], writes=[RB[b]])
                dst = hst[hs][:, j * TGS:(j + 1) * TGS]
                if j % 5 not in (0, 2):
                    S.op("vector", lambda e, b=b, j=j, dst=dst: e.tensor_scalar(
                        out=dst, in0=banks[b][:, 0:TGS], scalar1=Gcol[:, j:j + 1], scalar2=modcol[:, j:j + 1],
                        op0=ALU.mult, op1=ALU.add), reads=[RB[b], R_Gcol, R_modcol], writes=[R_hst[hs]])
                else:
                    S.op("scalar", lambda e, b=b, j=j, dst=dst: e.activation(
                        out=dst, in_=banks[b][:, 0:TGS], func=AF.Identity, bias=modcol[:, j:j + 1],
                        scale=Gcol[:, j:j + 1]), reads=[RB[b], R_Gcol, R_modcol], writes=[R_hstA[hs]])
            if js is None or js[-1] == KC - 1:
                S.dma("gpsimd", hscr_d[tg], hst[hs][:], reads=[R_hst[hs], R_hstA[hs]], writes=[R_hscr[tg]],
                      ch=CH_HS + hs)

        p1_prep(0)
        q4 = max(1, KC // 4)
        for tg in range(c.NTG):
            for ti in range(4):
                if tg + 1 < c.NTG:
                    p1_prep(tg + 1, [ti])
                js = list(range(ti * q4, KC if ti == 3 else (ti + 1) * q4))
                if js:
                    p1_tr(tg, js)
        S.barrier()

        C = Region(nc, PH0, SB_END)
        wbuf = [[C.alloc("wb%d_%d" % (a, b), [128, KC * 128], BF16) for b in range(4)] for a in range(2)]
        htile = [C.alloc("ht%d" % i, [128, KC * TGS], BF16) for i in range(2)]
        R_wbuf = [[Res("wb") for _ in range(4)] for _ in range(2)]
        R_ht = [Res("ht0"), Res("ht1")]
        PH2 = C.off
        CH_W = 4
        CH_HT = 6
        CH_YS = 12
        CH_WPL = 14
        CH_PY = 16
        ht_ctr = [0]
        pb_ctr = [0]
        R_yscr = [Res("yscr%d" % i) for i in range(c.NTGO)]

        def load_w(slot, cbs):
            for i, cb in enumerate(cbs):
                S.dma("gpsimd", wbuf[slot][i][:], win_d[cb], writes=[R_wbuf[slot][i]], ch=CH_W + slot * 4 + i)

        def load_ht(tg):
            hs = ht_ctr[0] % 2
            ht_ctr[0] += 1
            S.dma("sync", htile[hs][:], hscr_d[tg], reads=[R_hscr[tg]], writes=[R_ht[hs]], ch=CH_HT + hs)
            return hs

        def proj(slot, i, hs, ncols=TGS, col0=0):
            b = pb_ctr[0] % 3
            pb_ctr[0] += 1

            def fn(e):
                ins = None
                for k in range(KC):
                    ins = e.matmul(banks[b][:, 0:ncols], lhsT=wbuf[slot][i][:, k * 128:(k + 1) * 128],
                                   rhs=htile[hs][:, k * TGS + col0:k * TGS + col0 + ncols],
                                   start=(k == 0), stop=(k == KC - 1))
                return ins
            S.op("tensor", fn, reads=[R_wbuf[slot][i], R_ht[hs]], writes=[RB[b]])
            return b

        passes = []
        for h in range(NH):
            passes.append(("attn", h, [h, NH + h, 2 * NH + h, 3 * NH + h]))
        ub = 4 * NH
        gb = 4 * NH + c.DP // 128
        pool_passes = []
        for g in range(4):
            for c0 in range(0, CPG, 4):
                cc = list(range(c0, min(CPG, c0 + 4)))
                pool_passes.append(("pu", g, cc))
        for g in range(4):
            for c0 in range(0, CPG, 4):
                cc = list(range(c0, min(CPG, c0 + 4)))
                passes.append(("pu", (g, cc), [ub + g * CPG + x for x in cc]))
            for c0 in range(0, CPG, 4):
                cc = list(range(c0, min(CPG, c0 + 4)))
                passes.append(("pg", (g, cc), [gb + g * CPG + x for x in cc]))

        Aa = Region(nc, PH2, SB_END)
        KT = Aa.alloc("KT", [128, TT], BF16)
        QT = Aa.alloc("QT", [128, T], BF16)
        VT = Aa.alloc("VT", [128, TT], BF16)
        NBp = [T // (128 * d) for d in PATTERNS]
        vb_base = []
        nvb = 0
        for d, nb in zip(PATTERNS, NBp):
            vb_base.append(nvb)
            nvb += d * (nb + 1)
        NVBP = (nvb + 7) // 8 * 8
        Vb = Aa.alloc("Vb", [128, NVBP * 128], BF16)
        SG = Aa.alloc("SG", [128, T], BF16)
        oacc = Aa.alloc("oacc", [128, 2 * T], F32)
        tabs_sb = [Aa.alloc("tabs%d" % i, [128, 3 * 768], BF16) for i in range(2)]
        PTb = [Aa.alloc("PT%d" % i, [128, 256], BF16) for i in range(4)]
        yst = [Aa.alloc("yst0", [128, T], BF16)]
        yst.append(yst[0])
        R_KT, R_QT, R_VT, R_Vb, R_SG, R_oacc = Res("KT"), Res("QT"), Res("VT"), Res("Vb"), Res("SG"), Res("oacc")
        R_Vb2 = Res("Vb2")
        R_tabs = [Res("tabs0"), Res("tabs1")]
        R_PT = [Res("PT%d" % i) for i in range(4)]
        RS4 = [Res("S%d" % i) for i in range(4)]
        ROZ4 = [Res("OZ%d" % i) for i in range(4)]
        CH_TAB = 1
        R_yst = [Res("yst0")]
        R_yst.append(R_yst[0])
        SB_S = [3, 4, 0]
        SB_OZ = [5, 6, 1]
        SB_T = 7
        SC = 1.0 / math.sqrt(128.0)

        def sap(t, rowlen, start, step, count):
            return bass.AP(t, start, [[rowlen, 128], [step, count]])

        def attention_head(h, ys):
            ts = h % 2
            tb_ = tabs_sb[ts]
            vlist = []
            for pi, d in enumerate(PATTERNS):
                for r in range(d):
                    for n in range(-1, NBp[pi]):
                        vlist.append((HALO + 128 * n * d + r, d))
            for gi_, v0 in enumerate(range(0, len(vlist), 4)):
                grp = vlist[v0:v0 + 4]
                bT = (SB_T, 2, 5, 6)[gi_ % 4]

                def trv(e, grp=grp, bT=bT):
                    ins = None
                    for k, (start, d) in enumerate(grp):
                        ins = e.matmul(banks[bT][:, k * 128:(k + 1) * 128], lhsT=sap(VT, TT, start, d, 128),
                                       rhs=ident_bf[:], start=True, stop=True)
                    return ins
                S.op("tensor", trv, reads=[R_VT, R_ident], writes=[RB[bT]])
                n8 = len(grp)
                if gi_ % 2 == 0:
                    S.op("vector", lambda e, v0=v0, n8=n8, bT=bT: e.tensor_copy(
                        out=Vb[:, v0 * 128:(v0 + n8) * 128], in_=banks[bT][:, 0:n8 * 128]),
                        reads=[RB[bT]], writes=[R_Vb])
                else:
                    S.op("scalar", lambda e, v0=v0, n8=n8, bT=bT: e.activation(
                        out=Vb[:, v0 * 128:(v0 + n8) * 128], in_=banks[bT][:, 0:n8 * 128], func=AF.Copy),
                        reads=[RB[bT]], writes=[R_Vb2])
            blocks = []
            for pi, d in enumerate(PATTERNS):
                for r in range(d):
                    for n in range(NBp[pi]):
                        blocks.append((pi, d, r, n))

            def issue_S(bi):
                pi, d, r, n = blocks[bi]
                si = bi % 3
                bS, c0 = SB_S[si], 0
                kprev = sap(KT, TT, HALO + 128 * (n - 1) * d + r, d, 128)
                kcur = sap(KT, TT, HALO + 128 * n * d + r, d, 128)
                q = sap(QT, T, 128 * n * d + r, d, 128)
                t0 = pi * 768

                def mmS(e):
                    p0, p1 = ((t0 + 512, t0 + 640) if n == 0 else (t0, t0 + 256))
                    e.matmul(banks[bS][:, c0:c0 + 128], lhsT=kprev, rhs=q, start=True, stop=False)
                    e.matmul(banks[bS][:, c0:c0 + 128], lhsT=ident_bf[:], rhs=tb_[:, p0:p0 + 128],
                             start=False, stop=False)
                    e.matmul(banks[bS][:, c0:c0 + 128], lhsT=ident_bf[:], rhs=tb_[:, p1:p1 + 128],
                             start=False, stop=True)
                    e.matmul(banks[bS][:, c0 + 128:c0 + 256], lhsT=kcur, rhs=q, start=True, stop=False)
                    e.matmul(banks[bS][:, c0 + 128:c0 + 256], lhsT=ident_bf[:], rhs=tb_[:, t0 + 128:t0 + 256],
                             start=False, stop=False)
                    return e.matmul(banks[bS][:, c0 + 128:c0 + 256], lhsT=ident_bf[:],
                                    rhs=tb_[:, t0 + 384:t0 + 512], start=False, stop=True)
                S.op("tensor", mmS, reads=[R_KT, R_QT, R_tabs[ts], R_ident], writes=[RB[bS]])
                S.op("scalar", lambda e: e.activation(out=PTb[si][:], in_=banks[bS][:, c0:c0 + 256], func=AF.Exp,
                                                      scale=SC), reads=[RB[bS]], writes=[R_PT[si]])

            def issue_PV(bi):
                pi, d, r, n = blocks[bi]
                nb = NBp[pi]
                si = bi % 3
                bO, c0 = SB_OZ[si], 0
                vprev = vb_base[pi] + r * (nb + 1) + n
                vcur = vprev + 1

                def mmO(e):
                    e.matmul(banks[bO][:, c0:c0 + 128], lhsT=Vb[:, vprev * 128:(vprev + 1) * 128],
                             rhs=PTb[si][:, 0:128], start=True, stop=False)
                    e.matmul(banks[bO][:, c0:c0 + 128], lhsT=Vb[:, vcur * 128:(vcur + 1) * 128],
                             rhs=PTb[si][:, 128:256], start=False, stop=True)
                    e.matmul(banks[bO][:, c0 + 128:c0 + 256], lhsT=ones_bf[:], rhs=PTb[si][:, 0:128],
                             start=True, stop=False)
                    return e.matmul(banks[bO][:, c0 + 128:c0 + 256], lhsT=ones_bf[:], rhs=PTb[si][:, 128:256],
                                    start=False, stop=True)
                S.op("tensor", mmO, reads=[R_Vb, R_Vb2, R_PT[si], R_ones], writes=[RB[bO]])
                oap = bass.AP(oacc, 128 * n * d + r, [[2 * T, 128], [T, 2], [d, 128]])
                src = banks[bO][:, c0:c0 + 256].rearrange("p (a b) -> p a b", a=2)
                if pi == 0:
                    S.op("vector", lambda e: e.tensor_copy(out=oap, in_=src), reads=[RB[bO]], writes=[R_oacc])
                else:
                    S.op("vector", lambda e: e.tensor_tensor(out=oap, in0=oap, in1=src, op=ALU.add),
                         reads=[RB[bO], R_oacc], writes=[R_oacc])

            LOOK = 2
            for i in range(len(blocks) + LOOK):
                if i < len(blocks):
                    issue_S(i)
                if i - LOOK >= 0:
                    issue_PV(i - LOOK)
            S.op("vector", lambda e: e.reciprocal(out=oacc[:, T:2 * T], in_=oacc[:, T:2 * T]),
                 reads=[R_oacc], writes=[R_oacc])
            S.op("vector", lambda e: e.tensor_tensor(out=oacc[:, 0:T], in0=oacc[:, 0:T], in1=oacc[:, T:2 * T],
                                                     op=ALU.mult), reads=[R_oacc], writes=[R_oacc])
            S.op("vector", lambda e, ys=ys: e.tensor_tensor(out=yst[ys][:], in0=oacc[:, 0:T], in1=SG[:],
                                                            op=ALU.mult), reads=[R_oacc, R_SG], writes=[R_yst[ys]])
            ydst = yscr_d.rearrange("g p (m t) -> p g m t", t=TGS)[:, :, h, :]
            S.dma("gpsimd", ydst, yst[ys][:].rearrange("p (g t) -> p g t", t=TGS), reads=[R_yst[ys]],
                  writes=R_yscr, ch=CH_YS)

        def alloc_pool():
            Pp = Region(nc, PH2, SB_END)
            d = {}
            d["UT"] = [Pp.alloc("UT%d" % i, [128, 16 + T], F32) for i in range(CPG)]
            d["lv"] = [Pp.alloc("lv%d" % i, [128, 16 + T], F32) for i in range(2)]
            d["pooled"] = [Pp.alloc("pooled%d" % i, [128, T], BF16) for i in range(CPG)]
            d["sgp"] = [Pp.alloc("sgp%d" % i, [128, TGS], BF16) for i in range(3)]
            d["wpl"] = [Pp.alloc("wpl0", [128, CPG * CPG * 128], BF16)]
            d["wpl"].append(d["wpl"][0])
            d["pys"] = [Pp.alloc("pys%d" % i, [128, TGS], BF16) for i in range(2)]
            d["f16"] = Pp.alloc("f16", [128, 16], F32)
            return d

        npass = len(passes)
        E = Region(nc, PH0, SB_END)
        CGW, NCG, NT = c.CGW, c.NCG, c.NT
        wout, wout_off = [], []
        for i in range(2):
            wout.append(E.alloc("wout%d" % i, [128, MC * CGW], BF16))
            wout_off.append(E.last)
        ytile = [E.alloc("ytile%d" % i, [128, MC * TGS], BF16) for i in range(2)]
        R_wout = [Res("wout0"), Res("wout1")]
        R_yt = [Res("yt0"), Res("yt1")]
        CH_WO, CH_YT = 4, 1
        PREF3 = (npass % 2 == 0) and (MC * CGW <= 4 * KC * 128) and (MC == KC)
        load_w(0, passes[0][2])
        first_pool = True
        PB = None
        sgp_ctr = [0]
        for pidx, (kind, info, cbs) in enumerate(passes):
            slot = pidx % 2
            if pidx + 1 < npass:
                load_w(1 - slot, passes[pidx + 1][2])
            elif PREF3:
                S.dma("gpsimd", wout[0][:], wout_d[0], writes=[R_wout[0]] + R_wbuf[0], ch=CH_WO)
            if kind == "attn":
                h = info
                S.dma("sync", tabs_sb[h % 2][:], tabs_d[h], writes=[R_tabs[h % 2]], ch=CH_TAB + h % 2)
                tgs = list(range(c.NTG))
                hs_next = load_ht(tgs[0])
                for gi, tg in enumerate(tgs):
                    hs = hs_next
                    if gi + 1 < len(tgs):
                        hs_next = load_ht(tgs[gi + 1])
                    own = tg >= c.NTGH
                    t0 = tg * TGS
                    o0 = t0 - HALO
                    b = proj(slot, 1, hs)
                    S.op("scalar", lambda e, b=b, t0=t0: e.activation(out=KT[:, t0:t0 + TGS], in_=banks[b][:, 0:TGS],
                                                                      func=AF.Copy), reads=[RB[b]], writes=[R_KT])
                    b = proj(slot, 2, hs)
                    S.op("vector", lambda e, b=b, t0=t0: e.tensor_copy(out=VT[:, t0:t0 + TGS], in_=banks[b][:, 0:TGS]),
                         reads=[RB[b]], writes=[R_VT])
                    if own:
                        b = proj(slot, 0, hs)
                        S.op("vector", lambda e, b=b, o0=o0: e.tensor_copy(out=QT[:, o0:o0 + TGS],
                                                                           in_=banks[b][:, 0:TGS]),
                             reads=[RB[b]], writes=[R_QT])
                        b = proj(slot, 3, hs)
                        S.op("scalar", lambda e, b=b, o0=o0: e.activation(out=SG[:, o0:o0 + TGS],
                                                                          in_=banks[b][:, 0:TGS], func=AF.Silu),
                             reads=[RB[b]], writes=[R_SG])
                attention_head(h, h % 2)
            else:
                if first_pool:
                    S.barrier()
                    PB = alloc_pool()
                    R_UT = [Res("UT%d" % i) for i in range(CPG)]
                    R_lv = [Res("lv0"), Res("lv1")]
                    R_pooled = [Res("pooled%d" % i) for i in range(CPG)]
                    R_sgp = [Res("sgp%d" % i) for i in range(3)]
                    R_wpl = [Res("wpl0")]
                    R_wpl.append(R_wpl[0])
                    R_pys = [Res("pys%d" % i) for i in range(2)]
                    R_f16 = Res("f16")
                    first_pool = False
                g, ccs = info
                UT, lv, pooled, sgp, wpl, pys, f16 = (PB["UT"], PB["lv"], PB["pooled"], PB["sgp"], PB["wpl"],
                                                      PB["pys"], PB["f16"])
                if kind == "pu":
                    if ccs[0] == 0:
                        S.dma("gpsimd", wpl[g % 2][:], wpool_d[g], writes=[R_wpl[g % 2]], ch=CH_WPL)
                    tgs = [c.NTGH - 1] + list(range(c.NTGH, c.NTG))
                    hs_next = load_ht(tgs[0])
                    for gi, tg in enumerate(tgs):
                        hs = hs_next
                        if gi + 1 < len(tgs):
                            hs_next = load_ht(tgs[gi + 1])
                        for i, cc in enumerate(ccs):
                            if gi == 0:
                                b = proj(slot, i, hs, ncols=16, col0=TGS - 16)
                                S.op("vector", lambda e, b=b, cc=cc: e.tensor_scalar(
                                    out=UT[cc][:, 0:16], in0=banks[b][:, 0:16], scalar1=hflag[:, 0:1], scalar2=None,
                                    op0=ALU.mult), reads=[RB[b], R_const], writes=[R_UT[cc]])
                            else:
                                o0 = (tg - c.NTGH) * TGS
                                b = proj(slot, i, hs)
                                if (gi + i) % 2 == 0:
                                    S.op("vector", lambda e, b=b, cc=cc, o0=o0: e.tensor_copy(
                                        out=UT[cc][:, 16 + o0:16 + o0 + TGS], in_=banks[b][:, 0:TGS]),
                                        reads=[RB[b]], writes=[R_UT[cc]])
                                else:
                                    S.op("scalar", lambda e, b=b, cc=cc, o0=o0: e.activation(
                                        out=UT[cc][:, 16 + o0:16 + o0 + TGS], in_=banks[b][:, 0:TGS], func=AF.Copy),
                                        reads=[RB[b]], writes=[R_UT[cc]])
                    w = 2 ** (g + 1)
                    W_ = 16 + T
                    for cc in ccs:
                        src, rsrc = UT[cc], R_UT[cc]
                        cur, rcur = src, rsrc
                        for lvl in range(1, g + 2):
                            sh = 2 ** (lvl - 1)
                            v0 = 2 ** lvl - 1
                            dst, rdst = lv[(lvl - 1) % 2], R_lv[(lvl - 1) % 2]
                            S.op("vector", lambda e, dst=dst, cur=cur, v0=v0, sh=sh: e.tensor_tensor(
                                out=dst[:, v0:W_], in0=cur[:, v0:W_], in1=cur[:, v0 - sh:W_ - sh], op=ALU.add),
                                reads=[rcur], writes=[rdst])
                            cur, rcur = dst, rdst
                        S.op("vector", lambda e, cur=cur, src=src, cc=cc, w=w: e.scalar_tensor_tensor(
                            out=pooled[cc][:], in0=cur[:, 16:W_], scalar=1.0 / w, in1=src[:, 16:W_], op0=ALU.mult,
                            op1=ALU.subtract), reads=[rcur, rsrc], writes=[R_pooled[cc]])
                        S.op("vector", lambda e, cur=cur, g=g: e.tensor_tensor(
                            out=f16[:], in0=cur[:, 16:32], in1=invcnt[:, g * 16:(g + 1) * 16], op=ALU.mult),
                            reads=[rcur, R_const], writes=[R_f16])
                        S.op("vector", lambda e, src=src, cc=cc: e.tensor_tensor(
                            out=pooled[cc][:, 0:16], in0=f16[:], in1=src[:, 16:32], op=ALU.subtract),
                            reads=[R_f16, rsrc], writes=[R_pooled[cc]])
                else:
                    tgs = list(range(c.NTGH, c.NTG))
                    hs_next = load_ht(tgs[0])
                    for gi, tg in enumerate(tgs):
                        hs = hs_next
                        if gi + 1 < len(tgs):
                            hs_next = load_ht(tgs[gi + 1])
                        o0 = (tg - c.NTGH) * TGS
                        for i, dc in enumerate(ccs):
                            b = proj(slot, i, hs)
                            sg = sgp_ctr[0] % 3
                            py = sgp_ctr[0] % 2
                            sgp_ctr[0] += 1
                            S.op("scalar", lambda e, b=b, sg=sg: e.activation(out=sgp[sg][:], in_=banks[b][:, 0:TGS],
                                                                              func=AF.Silu),
                                 reads=[RB[b]], writes=[R_sgp[sg]])
                            b2 = pb_ctr[0] % 3
                            pb_ctr[0] += 1

                            def pmm(e, b2=b2, dc=dc, o0=o0, g=g):
                                ins = None
                                for cc in range(CPG):
                                    ins = e.matmul(banks[b2][:, 0:TGS],
                                                   lhsT=wpl[g % 2][:, (dc * CPG + cc) * 128:(dc * CPG + cc + 1) * 128],
                                                   rhs=pooled[cc][:, o0:o0 + TGS], start=(cc == 0),
                                                   stop=(cc == CPG - 1))
                                return ins
                            S.op("tensor", pmm, reads=R_pooled + [R_wpl[g % 2]], writes=[RB[b2]])
                            m = NH + g * CPG + dc
                            S.op("vector", lambda e, b2=b2, sg=sg, py=py, g=g, dc=dc: e.scalar_tensor_tensor(
                                out=pys[py][:], in0=banks[b2][:, 0:TGS], scalar=pscale[:, g * CPG + dc:g * CPG + dc + 1],
                                in1=sgp[sg][:], op0=ALU.mult, op1=ALU.mult),
                                reads=[RB[b2], R_sgp[sg], R_const], writes=[R_pys[py]])
                            tgo = tg - c.NTGH
                            S.dma("gpsimd", yscr_d[tgo][:, m * TGS:(m + 1) * TGS], pys[py][:], reads=[R_pys[py]],
                                  writes=[R_yscr[tgo]], ch=CH_PY + py)
        if PREF3:
            S.dma("sync", ytile[0][:], yscr_d[0], reads=[R_yscr[0]], writes=[R_yt[0], R_ht[0]], ch=CH_YT)
        S.barrier()

        gate_bc = E.alloc("gate_bc", [128, D], F32)
        fgain = E.alloc("fgain", [128, D], F32)
        NXT = 6
        xt3 = [E.alloc("xt%d" % i, [128, CGW], F32) for i in range(NXT)]
        xo3 = [E.alloc("xo%d" % i, [128, CGW], F32) for i in range(NXT)]
        junk3 = E.alloc("junk3", [128, CGW], BF16)
        ssq = E.alloc("ssq", [128, NT * NCG], F32)
        ssum = E.alloc("ssum", [128, NT], F32)
        rr = E.alloc("rr", [128, NT], F32)
        wfree = 1 - (NCG - 1) % 2
        assert NCG >= 2 and MC * CGW * 2 >= 2 * D * 4
        xr = [E.alloc("xr%d" % i, [128, D], F32, at=wout_off[wfree] + i * D * 4) for i in range(2)]
        xr.append(E.alloc("xr2", [128, D], F32))
        R_xr = [Res("xr0"), Res("xr1"), Res("xr2")]
        xr_used = [False, False, True]
        R_ssqt = [Res("ssq%d" % i) for i in range(NT)]
        R_g3 = Res("g3")
        R_xt = [Res("xt%d" % i) for i in range(NXT)]
        R_xo = [Res("xo%d" % i) for i in range(NXT)]
        R_junk3 = Res("junk3")
        R_out = [Res("out%d" % i) for i in range(NT)]
        CH_XO, CH_FS = 6, 12
        CH_XT, CH_XR, CH_G3 = 3, 9, 0
        S.dma("sync", gate_bc[:], gscr_d, reads=[R_gscr], writes=[R_g3], ch=CH_G3)
        S.dma("sync", fgain[:], fgain_d, writes=[R_g3], ch=CH_G3)
        S.op("vector", lambda e: e.memset(ssq[:], 0.0), writes=R_ssqt)
        it = 0
        rot = 0
        if not PREF3:
            S.dma("gpsimd", wout[0][:], wout_d[0], writes=[R_wout[0]], ch=CH_WO)

        def final_a(tl):
            s = tl % 3
            S.op("vector", lambda e: e.tensor_reduce(out=ssum[:, tl:tl + 1], in_=ssq[:, tl * NCG:(tl + 1) * NCG],
                                                     axis=AX.X, op=ALU.add), reads=[R_ssqt[tl]], writes=[R_ssqt[tl]])
            S.op("scalar", lambda e: e.activation(out=rr[:, tl:tl + 1], in_=ssum[:, tl:tl + 1], func=AF.Sqrt,
                                                  bias=EPS, scale=1.0 / D), reads=[R_ssqt[tl]], writes=[R_ssqt[tl]])
            S.op("vector", lambda e: e.reciprocal(out=rr[:, tl:tl + 1], in_=rr[:, tl:tl + 1]),
                 reads=[R_ssqt[tl]], writes=[R_ssqt[tl]])
            wr = [R_xr[s]] + ([] if xr_used[s] else [R_wout[wfree]])
            xr_used[s] = True
            S.dma("scalar", xr[s][:], out_d[tl * 128:(tl + 1) * 128, :], reads=[R_out[tl]], writes=wr, ch=CH_XR + s)

        def final_b(tl):
            s = tl % 3
            S.op("vector", lambda e: e.scalar_tensor_tensor(
                out=xr[s][:], in0=xr[s][:], scalar=rr[:, tl:tl + 1], in1=fgain[:], op0=ALU.mult, op1=ALU.mult),
                reads=[R_xr[s], R_ssqt[tl], R_g3], writes=[R_xr[s]])
            S.dma("gpsimd", out_d[tl * 128:(tl + 1) * 128, :], xr[s][:], reads=[R_xr[s]], writes=[R_out[tl]],
                  ch=CH_FS + s)
        for cg in range(NCG):
            ws = cg % 2
            if cg + 1 < NCG:
                S.dma("gpsimd", wout[1 - ws][:], wout_d[cg + 1], writes=[R_wout[1 - ws]], ch=CH_WO + 1 - ws)
            for tg in range(c.NTGO):
                ysl = it % 2
                if it == 0 and not PREF3:
                    S.dma("sync", ytile[0][:], yscr_d[0], reads=[R_yscr[0]], writes=[R_yt[0]], ch=CH_YT)
                it += 1
                if it < NCG * c.NTGO:
                    ntg = it % c.NTGO
                    S.dma("sync", ytile[1 - ysl][:], yscr_d[ntg], reads=[R_yscr[ntg]], writes=[R_yt[1 - ysl]],
                          ch=CH_YT + 1 - ysl)
                for ti in range(4):
                    tl = tg * 4 + ti
                    ro = rot % NXT
                    rot += 1
                    S.dma("sync", xt3[ro][:], x_d[HALO + tl * 128:HALO + (tl + 1) * 128, cg * CGW:(cg + 1) * CGW],
                          writes=[R_xt[ro]], ch=CH_XT + ro)
                    b = pb_ctr[0] % 3
                    pb_ctr[0] += 1

                    def omm(e, b=b, ysl=ysl, ws=ws, ti=ti):
                        ins = None
                        for m in range(MC):
                            ins = e.matmul(banks[b][:, 0:CGW],
                                           lhsT=ytile[ysl][:, m * TGS + ti * 128:m * TGS + (ti + 1) * 128],
                                           rhs=wout[ws][:, m * CGW:(m + 1) * CGW], start=(m == 0), stop=(m == MC - 1))
                        return ins
                    S.op("tensor", omm, reads=[R_yt[ysl], R_wout[ws]], writes=[RB[b]])
                    gsl = slice(cg * CGW, (cg + 1) * CGW)
                    S.op("vector", lambda e, b=b, ro=ro, gsl=gsl: e.tensor_tensor(
                        out=xo3[ro][:], in0=banks[b][:, 0:CGW], in1=gate_bc[:, gsl], op=ALU.mult),
                        reads=[RB[b], R_g3], writes=[R_xo[ro]])
                    S.op("vector", lambda e, ro=ro: e.tensor_tensor(out=xo3[ro][:], in0=xo3[ro][:], in1=xt3[ro][:],
                                                                    op=ALU.add),
                         reads=[R_xo[ro], R_xt[ro]], writes=[R_xo[ro]])
                    col = tl * NCG + cg
                    S.op("scalar", lambda e, ro=ro, col=col: e.activation(out=junk3[:], in_=xo3[ro][:], func=AF.Square,
                                                                          accum_out=ssq[:, col:col + 1]),
                         reads=[R_xo[ro], R_ssqt[tl]], writes=[R_junk3, R_ssqt[tl]])
                    S.dma("gpsimd", out_d[tl * 128:(tl + 1) * 128, cg * CGW:(cg + 1) * CGW], xo3[ro][:],
                          reads=[R_xo[ro]], writes=[R_out[tl]], ch=CH_XO + ro)
                    if cg == NCG - 1:
                        if tl >= 1:
                            final_a(tl - 1)
                        if tl >= 2:
                            final_b(tl - 2)
        final_a(NT - 1)
        final_b(NT - 2)
        final_b(NT - 1)
        S.final_wait("sync", R_out)
        S.final_wait("gpsimd", R_out)
        S.emit(block)
    return nc


def make_tables():
    kp = np.arange(128)[:, None]
    qi = np.arange(128)[None, :]
    rel0 = 128 + qi - kp
    rel1 = qi - kp
    relT = np.concatenate([rel0, rel1], axis=1).astype(np.float32)
    valid = np.concatenate([(rel0 >= 0) & (rel0 <= 128), (rel1 >= 0) & (rel1 <= 128)], axis=1)
    maskT = np.where(valid, 0.0, NEG).astype(np.float32)
    relT = np.where(valid, relT, 0.0).astype(np.float32)
    return relT, maskT


def make_tabs(NH, first):
    import ml_dtypes
    relT, maskT = make_tables()
    SC = 1.0 / math.sqrt(128.0)
    out = np.zeros((NH, 128, 3 * 768), dtype=ml_dtypes.bfloat16)
    hm = np.zeros((128, 256), np.float32)
    if first:
        hm[:, 0:128] = NEG

    def split(v):
        hi = v.astype(ml_dtypes.bfloat16)
        lo = (v - hi.astype(np.float32)).astype(ml_dtypes.bfloat16)
        return hi, lo
    for h in range(NH):
        slope = 2.0 ** (-8.0 * (h + 1) / NH)
        for pi, d in enumerate(PATTERNS):
            v = ((relT * np.float32(-slope * d) + maskT) / np.float32(SC)).astype(np.float32)
            vf = ((relT * np.float32(-slope * d) + maskT + hm) / np.float32(SC)).astype(np.float32)
            hi, lo = split(v)
            fhi, flo = split(vf)
            o = pi * 768
            out[h, :, o:o + 256] = hi
            out[h, :, o + 256:o + 512] = lo
            out[h, :, o + 512:o + 640] = fhi[:, 0:128]
            out[h, :, o + 640:o + 768] = flo[:, 0:128]
    return out


def core_map(B, nchunk):
    npair = B * nchunk // 2
    out = []
    for core in range(2 * npair):
        p, mem = core % npair, core // npair
        out.append((p // (nchunk // 2), 2 * (p % (nchunk // 2)) + mem, mem))
    return out


def col_layout(v):
    v = np.asarray(v, np.float32)
    return np.ascontiguousarray(v.reshape(-1, 128).T)


def prep_inputs(cfg, x, c, norm_gain, w_ada, b_ada, w_in, w_pool, pool_scale, w_out, final_gain, seq_chunks):
    cf = cfg
    D, KC, T, HALO, MC, CPG = cf.D, cf.KC, cf.T, cf.HALO, cf.MC, cf.CPG
    B = x.shape[0]
    relT, maskT = make_tables()
    w_ada0 = np.asarray(w_ada[0], np.float32)
    b = np.asarray(b_ada[0], np.float32)
    CWQ = cf.CPR * 128
    wq = [np.ascontiguousarray(w_ada0[:, r * CWQ:(r + 1) * CWQ]) for r in range(2)]
    bq = [col_layout(b[r * CWQ:(r + 1) * CWQ]) for r in range(2)]
    shared = {
        "gain_col": col_layout(norm_gain[0]),
        "w_in_r": np.ascontiguousarray(
            np.asarray(w_in[0], np.float32).reshape(KC, 128, cf.NCB, 128).transpose(2, 1, 0, 3)).reshape(
                cf.NCB, 128, KC * 128),
        "w_pool_r": np.ascontiguousarray(
            np.asarray(w_pool[0], np.float32).reshape(4, CPG, 128, CPG, 128).transpose(0, 2, 3, 1, 4)).reshape(
                4, 128, CPG * CPG * 128),
        "pscale_col": col_layout(pool_scale[0]),
        "w_out_r": np.ascontiguousarray(
            np.asarray(w_out[0], np.float32).reshape(MC, 128, cf.NCG, cf.CGW).transpose(2, 1, 0, 3)).reshape(
                cf.NCG, 128, MC * cf.CGW),
        "fgain_bc": np.ascontiguousarray(np.broadcast_to(np.asarray(final_gain, np.float32)[None, :], (128, D))),
        "relT": relT, "maskT": maskT, "ident": np.eye(128, dtype=np.float32),
    }
    in_maps = []
    tabs_first = make_tabs(cf.NH, True)
    tabs_rest = make_tabs(cf.NH, False)
    for bi, ci, mem in core_map(B, seq_chunks):
        if True:
            cT = col_layout(c[bi])
            first = ci == 0
            xo = np.asarray(x[bi, ci * T:(ci + 1) * T], np.float32)
            if first:
                xh = np.zeros((HALO, D), np.float32)
            else:
                xh = np.asarray(x[bi, ci * T - HALO:ci * T], np.float32)
            hflag = np.empty((128, 2), np.float32)
            hflag[:, 0] = 0.0 if first else 1.0
            hflag[:, 1] = NEG if first else 0.0
            inv = np.empty((4, 16), np.float32)
            for g in range(4):
                w = 2 ** (g + 1)
                if first:
                    inv[g] = 1.0 / np.minimum(np.arange(1, 17), w)
                else:
                    inv[g] = 1.0 / w
            m = dict(shared)
            m["x"] = np.concatenate([xh, xo], axis=0)
            m["cT"] = cT
            m["hflag"] = hflag
            m["w_ada_q"] = wq[mem]
            m["b_ada_col"] = bq[mem]
            m["tabs"] = tabs_first if first else tabs_rest
            m["invcnt"] = np.ascontiguousarray(np.broadcast_to(inv.reshape(1, 64), (128, 64)))
            in_maps.append(m)
    return in_maps


_NC_CACHE = {}


def kernel(x, c, norm_gain, w_ada, b_ada, w_in, w_pool, pool_scale, w_out, final_gain):
    x = np.asarray(x)
    B, SEQ, D = x.shape
    nchunk = SEQ // 2048
    cfg = Cfg(NPAIR=B * nchunk // 2)
    in_maps = prep_inputs(cfg, x, np.asarray(c), np.asarray(norm_gain), np.asarray(w_ada), np.asarray(b_ada),
                          np.asarray(w_in), np.asarray(w_pool), np.asarray(pool_scale), np.asarray(w_out),
                          np.asarray(final_gain), nchunk)
    if "nc" not in _NC_CACHE:
        _NC_CACHE["nc"] = build_nc(cfg)
    nc = _NC_CACHE["nc"]
    n = len(in_maps)
    res = run_bass_kernel_spmd(nc, in_maps, core_ids=list(range(n)))
    out = np.empty((B, SEQ, D), np.float32)
    for k, (bi, ci, mem) in enumerate(core_map(B, nchunk)):
        out[bi, ci * cfg.T:(ci + 1) * cfg.T] = np.asarray(res.results[k]["out"], np.float32)
    return out
```
